# Optimizing a Trainium2 kernel written in Bass

```python
import jax, jax.numpy as jnp
from jax import lax
import numpy as np

D_MODEL = 2048
BATCH = 2
SEQ = 4096
DEPTH = 1

N_MEM = 256
RET_HEADS = 8
RET_QK_DIM = 128
RET_V_DIM = 256
D_RET_QK = RET_HEADS * RET_QK_DIM
D_RET_V = RET_HEADS * RET_V_DIM
RET_CHUNK = 128
RNN_WIDTH = D_MODEL
RNN_BLOCKS = 16
RNN_BLOCK = RNN_WIDTH // RNN_BLOCKS
RNN_CONV = 4
RG_C = 8.0
XA_HEADS = 4
XA_HEAD_DIM = D_MODEL // XA_HEADS
D_FF = 5632
FFN_CONV = 3
ROPE_BASE = 10000.0
NORM_EPS = 1e-6
GN_EPS = 1e-5
MAX_POS_OFFSET = 1024

kernel_name = "hybrid_retention_rglru_gated_merge"


def rmsnorm(x, w):
    xf = x.astype(jnp.float32)
    y = xf * lax.rsqrt(jnp.mean(xf * xf, axis=-1, keepdims=True) + NORM_EPS)
    return (y * w.astype(jnp.float32)).astype(x.dtype)


def rotary(x, positions):
    d = x.shape[-1]
    half = d // 2
    inv_freq = ROPE_BASE ** (-jnp.arange(0, half, dtype=jnp.float32) * 2.0 / d)
    ang = positions.astype(jnp.float32)[..., None] * inv_freq
    cos = jnp.cos(ang)[:, :, None, :].astype(x.dtype)
    sin = jnp.sin(ang)[:, :, None, :].astype(x.dtype)
    x1, x2 = x[..., :half], x[..., half:]
    return jnp.concatenate([x1 * cos - x2 * sin, x2 * cos + x1 * sin], axis=-1)


def causal_dwconv(x, w, bias):
    k, c = w.shape
    y = lax.conv_general_dilated(
        x, w[:, None, :].astype(x.dtype), window_strides=(1,), padding=((k - 1, 0),),
        dimension_numbers=("NWC", "WIO", "NWC"), feature_group_count=c)
    return y + bias.astype(x.dtype)


def chunkwise_retention(q, k, v):
    dtype = v.dtype
    q = q.astype(jnp.float32)
    k = k.astype(jnp.float32)
    v = v.astype(jnp.float32)
    b, s, h, dk = q.shape
    dv = v.shape[-1]
    n = RET_CHUNK
    c = s // n
    log_gamma = jnp.log1p(-(2.0 ** (-5.0 - jnp.arange(h, dtype=jnp.float32))))
    idx = jnp.arange(n, dtype=jnp.float32)
    rel = idx[:, None] - idx[None, :]
    causal = rel >= 0
    d_intra = jnp.where(causal[None], jnp.exp(log_gamma[:, None, None] * jnp.where(causal, rel, 0.0)[None]), 0.0)
    q_decay = jnp.exp(log_gamma[None, :] * (idx[:, None] + 1.0))
    k_decay = jnp.exp(log_gamma[None, :] * (n - 1.0 - idx[:, None]))
    chunk_decay = jnp.exp(log_gamma * n)

    qc = q.reshape(b, c, n, h, dk)
    kc = k.reshape(b, c, n, h, dk)
    vc = v.reshape(b, c, n, h, dv)
    scores = jnp.einsum("bcnhd,bcmhd->bchnm", qc, kc) * d_intra
    inner = jnp.einsum("bchnm,bcmhe->bcnhe", scores, vc)
    kv = jnp.einsum("bcmhd,bcmhe->bchde", kc * k_decay[None, None, :, :, None], vc)

    def step(state, kv_c):
        return state * chunk_decay[None, :, None, None] + kv_c, state

    init = jnp.zeros((b, h, dk, dv), jnp.float32)
    _, prev = lax.scan(step, init, jnp.moveaxis(kv, 1, 0))
    prev = jnp.moveaxis(prev, 0, 1)
    cross = jnp.einsum("bcnhd,bchde->bcnhe", qc * q_decay[None, None, :, :, None], prev)
    return (inner + cross).reshape(b, s, h, dv).astype(dtype)


def head_groupnorm(x, w):
    xf = x.astype(jnp.float32)
    mu = jnp.mean(xf, axis=-1, keepdims=True)
    xc = xf - mu
    var = jnp.mean(xc * xc, axis=-1, keepdims=True)
    y = xc * lax.rsqrt(var + GN_EPS) * w.astype(jnp.float32).reshape(x.shape[2], x.shape[3])
    return y.astype(x.dtype)


def rg_lru(xr, w_a, b_a, w_x, b_x, lam):
    b, s, w = xr.shape
    xb = xr.reshape(b, s, RNN_BLOCKS, RNN_BLOCK)
    gate_r = jax.nn.sigmoid(jnp.einsum("bsgi,gij->bsgj", xb, w_a).reshape(b, s, w) + b_a)
    gate_i = jax.nn.sigmoid(jnp.einsum("bsgi,gij->bsgj", xb, w_x).reshape(b, s, w) + b_x)
    log_a = -RG_C * gate_r.astype(jnp.float32) * jax.nn.softplus(-lam.astype(jnp.float32))
    a = jnp.exp(log_a)
    u = jnp.sqrt(-jnp.expm1(2.0 * log_a)) * (gate_i * xr).astype(jnp.float32)

    def combine(left, right):
        a_l, b_l = left
        a_r, b_r = right
        return a_l * a_r, a_r * b_l + b_r

    _, hseq = lax.associative_scan(combine, (a, u), axis=1)
    return hseq.astype(xr.dtype)


def memory_cross_attention(hn, mem_n, w_q, w_k, w_v, w_o):
    b, s, _ = hn.shape
    m = mem_n.shape[1]
    q = (hn @ w_q).reshape(b, s, XA_HEADS, XA_HEAD_DIM)
    k = (mem_n @ w_k).reshape(b, m, XA_HEADS, XA_HEAD_DIM)
    v = (mem_n @ w_v).reshape(b, m, XA_HEADS, XA_HEAD_DIM)
    logits = jnp.einsum("bshd,bmhd->bhsm", q, k).astype(jnp.float32) * (XA_HEAD_DIM ** -0.5)
    p = jax.nn.softmax(logits, axis=-1).astype(v.dtype)
    o = jnp.einsum("bhsm,bmhd->bshd", p, v).reshape(b, s, D_MODEL)
    return o @ w_o


def conv_gated_ffn(hn, w_up, conv_w, conv_b, w_down):
    up = causal_dwconv(hn @ w_up, conv_w, conv_b)
    gate, val = jnp.split(up, 2, axis=-1)
    return (jax.nn.silu(gate) * val) @ w_down


def setup_inputs(seed: int = 0) -> dict:
    key = jax.random.key(seed)
    ks = jax.random.split(key, 32)
    f32 = jnp.float32
    d_in = 2 * D_RET_QK + 2 * D_RET_V + 2 * RNN_WIDTH + 2 * D_MODEL

    def dense(k, shape, fan_in, scale=1.0):
        return jax.random.normal(k, shape, f32) * (scale * fan_in ** -0.5)

    def gain(k, shape):
        return 1.0 + 0.02 * jax.random.normal(k, shape, f32)

    def bias(k, shape):
        return 0.01 * jax.random.normal(k, shape, f32)

    u = jax.random.uniform(ks[10], (DEPTH, RNN_WIDTH), f32, 0.9, 0.999)
    start = jax.random.randint(ks[2], (BATCH, 1), 0, MAX_POS_OFFSET, dtype=jnp.int32)
    return {
        "x": jax.random.normal(ks[0], (BATCH, SEQ, D_MODEL), f32),
        "mem": jax.random.normal(ks[1], (BATCH, N_MEM, D_MODEL), f32),
        "positions": start + jnp.arange(SEQ, dtype=jnp.int32)[None, :],
        "norm_mix_w": gain(ks[3], (DEPTH, D_MODEL)),
        "w_in": dense(ks[4], (DEPTH, D_MODEL, d_in), D_MODEL),
        "ret_gn_w": gain(ks[5], (DEPTH, D_RET_V)),
        "rnn_conv_w": dense(ks[6], (DEPTH, RNN_CONV, RNN_WIDTH), RNN_CONV),
        "rnn_conv_b": bias(ks[7], (DEPTH, RNN_WIDTH)),
        "rg_w_a": dense(ks[8], (DEPTH, RNN_BLOCKS, RNN_BLOCK, RNN_BLOCK), RNN_BLOCK),
        "rg_b_a": bias(ks[9], (DEPTH, RNN_WIDTH)),
        "rg_w_x": dense(ks[11], (DEPTH, RNN_BLOCKS, RNN_BLOCK, RNN_BLOCK), RNN_BLOCK),
        "rg_b_x": bias(ks[12], (DEPTH, RNN_WIDTH)),
        "rg_lambda": jnp.log(u) - jnp.log1p(-u),
        "w_proj_ret": dense(ks[13], (DEPTH, D_RET_V, D_MODEL), D_RET_V),
        "w_proj_rnn": dense(ks[14], (DEPTH, RNN_WIDTH, D_MODEL), RNN_WIDTH),
        "w_mix_out": dense(ks[15], (DEPTH, D_MODEL, D_MODEL), D_MODEL, 0.5),
        "norm_xa_w": gain(ks[16], (DEPTH, D_MODEL)),
        "norm_mem_w": gain(ks[17], (DEPTH, D_MODEL)),
        "xa_w_q": dense(ks[18], (DEPTH, D_MODEL, D_MODEL), D_MODEL),
        "xa_w_k": dense(ks[19], (DEPTH, D_MODEL, D_MODEL), D_MODEL),
        "xa_w_v": dense(ks[20], (DEPTH, D_MODEL, D_MODEL), D_MODEL),
        "xa_w_o": dense(ks[21], (DEPTH, D_MODEL, D_MODEL), D_MODEL, 0.5),
        "norm_ffn_w": gain(ks[22], (DEPTH, D_MODEL)),
        "ffn_w_up": dense(ks[23], (DEPTH, D_MODEL, 2 * D_FF), D_MODEL),
        "ffn_conv_w": dense(ks[24], (DEPTH, FFN_CONV, 2 * D_FF), FFN_CONV),
        "ffn_conv_b": bias(ks[25], (DEPTH, 2 * D_FF)),
        "ffn_w_down": dense(ks[26], (DEPTH, D_FF, D_MODEL), D_FF, 0.5),
        "final_norm_w": gain(ks[27], (D_MODEL,)),
    }


def reference(x, mem, positions, norm_mix_w, w_in, ret_gn_w, rnn_conv_w, rnn_conv_b,
              rg_w_a, rg_b_a, rg_w_x, rg_b_x, rg_lambda, w_proj_ret, w_proj_rnn, w_mix_out,
              norm_xa_w, norm_mem_w, xa_w_q, xa_w_k, xa_w_v, xa_w_o,
              norm_ffn_w, ffn_w_up, ffn_conv_w, ffn_conv_b, ffn_w_down, final_norm_w):
    b, s, _ = x.shape
    sizes = [D_RET_QK, D_RET_QK, D_RET_V, D_RET_V, RNN_WIDTH, RNN_WIDTH, D_MODEL, D_MODEL]
    offsets = [int(o) for o in np.cumsum(sizes)[:-1]]
    h = x
    for l in range(DEPTH):
        hn = rmsnorm(h, norm_mix_w[l])
        q, k, v, g_ret, xr, yr, g_a, g_b = jnp.split(hn @ w_in[l], offsets, axis=-1)
        q = rotary(q.reshape(b, s, RET_HEADS, RET_QK_DIM), positions)
        k = rotary(k.reshape(b, s, RET_HEADS, RET_QK_DIM), positions) * (RET_QK_DIM ** -0.5)
        v = v.reshape(b, s, RET_HEADS, RET_V_DIM)
        ret = head_groupnorm(chunkwise_retention(q, k, v), ret_gn_w[l]).reshape(b, s, D_RET_V)
        ret_out = jax.nn.silu(g_ret) * ret
        xr = causal_dwconv(xr, rnn_conv_w[l], rnn_conv_b[l])
        hr = rg_lru(xr, rg_w_a[l], rg_b_a[l], rg_w_x[l], rg_b_x[l], rg_lambda[l])
        rnn_out = jax.nn.gelu(yr) * hr
        merged = (jax.nn.sigmoid(g_a) * (ret_out @ w_proj_ret[l])
                  + jax.nn.sigmoid(g_b) * (rnn_out @ w_proj_rnn[l]))
        h = h + merged @ w_mix_out[l]
        hn = rmsnorm(h, norm_xa_w[l])
        mem_n = rmsnorm(mem, norm_mem_w[l])
        h = h + memory_cross_attention(hn, mem_n, xa_w_q[l], xa_w_k[l], xa_w_v[l], xa_w_o[l])
        hn = rmsnorm(h, norm_ffn_w[l])
        h = h + conv_gated_ffn(hn, ffn_w_up[l], ffn_conv_w[l], ffn_conv_b[l], ffn_w_down[l])
    return rmsnorm(h, final_norm_w)
```

```python
import contextlib
import os
import numpy as np
import concourse.bass as bass
import concourse.mybir as mybir
from concourse.bass_utils import run_bass_kernel_spmd

F32 = mybir.dt.float32
BF16 = mybir.dt.bfloat16
I32 = mybir.dt.int32
U8 = mybir.dt.uint8
AF = mybir.ActivationFunctionType
OP = mybir.AluOpType

T = 1024
NT = 8
D = 2048
KC = 16
DFF = 5632
NFC = 44
STOP = int(os.environ.get("KSTOP", "-1"))


SMALLW = STOP in (-3, -2, 0, 20, 21, 22, 23)


class _Stop(Exception):
    pass
H = 8
TWO_PI = 6.283185307179586
C1 = 6.28125
C2 = TWO_PI - C1

V_NMIX, V_CW, V_CB, V_BA, V_BX, V_LAM, V_NXA, V_NMEM, V_NFFN, V_GN, V_FCW, V_FCB = (
    0, 16, 80, 96, 112, 128, 144, 160, 176, 192, 208, 472)
NV = 560


class Buf:
    __slots__ = ("name", "w", "r", "excl")

    def __init__(self, name="", excl=False):
        self.name = name
        self.w = None
        self.r = []
        self.excl = excl


class Sched:
    ENGS = ("tensor", "vector", "scalar", "gpsimd", "sync")

    def __init__(self, nc, n_dma_sems=24, same_engine_sync=True):
        self.nc = nc
        self.same_engine_sync = same_engine_sync
        self.prog = {e: [] for e in self.ENGS}
        self.cnt = {e: 0 for e in self.ENGS}
        self.seen = {e: {} for e in self.ENGS}
        self.n_dma_sems = n_dma_sems
        self.dma_val = [0] * n_dma_sems
        self.dma_rr = 0
        self.sw_rr = 0

    def _need(self, eng, toks):
        seen = self.seen[eng]
        best = {}
        for t in toks:
            if t is None:
                continue
            k, v = t
            if k == eng and (eng == "tensor" or not self.same_engine_sync):
                continue
            if seen.get(k, 0) >= v:
                continue
            if best.get(k, 0) < v:
                best[k] = v
        for k, v in best.items():
            seen[k] = v
        return list(best.items())

    @staticmethod
    def _deps(reads, writes, eng=None):
        toks = []
        for b in reads:
            toks.append(b.w)
            if b.excl:
                toks.extend(t for t in b.r if t[0] != eng)
        for b in writes:
            toks.append(b.w)
            toks.extend(b.r)
        return toks

    @staticmethod
    def _commit(tok, reads, writes):
        for b in reads:
            b.r.append(tok)
        for b in writes:
            b.w = tok
            b.r = []

    def op(self, eng, fn, reads=(), writes=()):
        waits = self._need(eng, self._deps(reads, writes, eng))
        self.cnt[eng] += 1
        tok = (eng, self.cnt[eng])
        self.prog[eng].append((waits, fn, ("eng", eng)))
        self._commit(tok, reads, writes)
        return tok

    def dma(self, q, fn, reads=(), writes=()):
        half = self.n_dma_sems // 2
        if q == "gpsimd":
            s = self.sw_rr
            self.sw_rr = (self.sw_rr + 1) % half
        else:
            s = half + self.dma_rr
            self.dma_rr = (self.dma_rr + 1) % (self.n_dma_sems - half)
        key = ("dma", s)
        toks = self._deps(reads, writes, q)
        if self.dma_val[s] > 0:
            toks.append((key, self.dma_val[s]))
        waits = self._need(q, toks)
        self.dma_val[s] += 16
        tok = (key, self.dma_val[s])
        self.prog[q].append((waits, fn, ("dma", s)))
        self._commit(tok, reads, writes)
        return tok

    def barrier(self):
        toks = [(e, self.cnt[e]) for e in self.ENGS if self.cnt[e] > 0]
        toks += [(("dma", s), v) for s, v in enumerate(self.dma_val) if v > 0]
        for e in self.ENGS:
            waits = self._need(e, toks)
            if waits:
                self.prog[e].append((waits, None, None))

    def emit(self):
        nc = self.nc
        with contextlib.ExitStack() as st:
            esem = {e: st.enter_context(nc.semaphore("s_" + e)) for e in self.ENGS}
            dsem = [st.enter_context(nc.semaphore("d_%d" % i)) for i in range(self.n_dma_sems)]

            def semof(k):
                return esem[k] if isinstance(k, str) else dsem[k[1]]

            block = st.enter_context(nc.Block())

            def body(engname):
                def run(e):
                    for waits, fn, inc in self.prog[engname]:
                        for k, v in waits:
                            e.wait_ge(semof(k), v)
                        if fn is None:
                            continue
                        ins = fn(e)
                        if inc[0] == "eng":
                            ins.then_inc(esem[inc[1]], 1)
                        else:
                            ins.then_inc(dsem[inc[1]], 16)
                return run

            block.tensor(body("tensor"))
            block.vector(body("vector"))
            block.scalar(body("scalar"))
            block.gpsimd(body("gpsimd"))
            block.sync(body("sync"))


_DS = {F32: 4, BF16: 2, I32: 4, U8: 1}


class Arena:
    def __init__(self, t, size):
        self.t = t
        self.size = size
        self.off = 0

    def alloc(self, shape, dtype):
        n = int(np.prod(shape))
        nb = n * _DS[dtype]
        off = (self.off + 63) // 64 * 64
        assert off + nb <= self.size, ("arena overflow", off, nb, self.size)
        v = self.t[:, off:off + nb]
        if dtype != U8:
            v = v.bitcast(dtype)
        if len(shape) == 2:
            v = v.rearrange("p (a b) -> p a b", a=shape[0], b=shape[1])
        elif len(shape) == 3:
            v = v.rearrange("p (a b c) -> p a b c", a=shape[0], b=shape[1], c=shape[2])
        self.off = off + nb
        return v


def build_nc():
    nc = bass.Bass("TRN2", target_bir_lowering=False)

    def din(name, shape, dt=F32):
        return nc.dram_tensor(name, list(shape), dt, kind="ExternalInput").ap()

    x_d = din("x", [T, D])
    xh_d = din("xh", [128, D])
    mem_d = din("mem", [256, D])
    pos_d = din("pos", [128, NT], I32)
    vecs_d = din("vecs", [128, NV])
    invf_d = din("invf", [128, 64])
    dt_d = din("dtab", [128, H * 128])
    qdec_d = din("qdec", [128, H * 128])
    kdec_d = din("kdec", [128, H])
    coef_d = din("coefr", [128, 32])
    mskl_d = din("maskl", [128, 4])
    mskp_d = din("maskp", [128, 4])
    ident_d = din("ident", [128, 128])
    fnw_d = din("fnw", [128, D])
    _din = din

    def din(name, shape, dt=F32):
        return _din(name, [1, 1] if SMALLW else shape, dt)
    w_in = _din("w_in", [D, 14336]) if STOP >= 20 else din("w_in", [D, 14336])
    rg_wa = din("rg_w_a", [16, 128, 128])
    rg_wx = din("rg_w_x", [16, 128, 128])
    w_pret = din("w_proj_ret", [D, D])
    w_prnn = din("w_proj_rnn", [D, D])
    w_mix = din("w_mix_out", [D, D])
    w_xq = din("xa_w_q", [D, D])
    w_xk = din("xa_w_k", [D, D])
    w_xv = din("xa_w_v", [D, D])
    w_xo = din("xa_w_o", [D, D])
    w_up = din("ffn_w_up", [D, 2 * DFF])
    w_dn = din("ffn_w_down", [DFF, D])
    out_d = nc.dram_tensor("out", [T, D], F32, kind="ExternalOutput").ap()

    gin1 = nc.dram_tensor("gin1", [H * 128, 256], F32)
    gout1 = nc.dram_tensor("gout1", [4 * H * 128, 256], F32)
    gin2 = nc.dram_tensor("gin2", [128, 32], F32)
    gout2 = nc.dram_tensor("gout2", [4 * 128, 32], F32)
    gin3 = nc.dram_tensor("gin3", [128, 32], F32)
    gout3 = nc.dram_tensor("gout3", [4 * 128, 32], F32)
    RG = [[0, 1, 2, 3], [4, 5, 6, 7]]

    gam = [1.0 - 2.0 ** (-5.0 - h) for h in range(H)]

    ARENA = 207 * 1024
    st = contextlib.ExitStack()
    with st:
      try:
        arena_t = st.enter_context(nc.sbuf_tensor("arena", [128, ARENA], U8))
        psum_t = st.enter_context(nc.psum_tensor("ps", [128, 4096], F32))
        S = Sched(nc)
        A = Arena(arena_t, ARENA)
        ps = psum_t[:, :]
        psb = psum_t[:, :].bitcast(BF16)

        def bank(i, n=512, off=0):
            return ps[:, i * 512 + off: i * 512 + off + n]

        pbuf = [[Buf("ps%d" % i, excl=True)] * 4 for i in range(8)]

        ident = A.alloc([128], BF16)
        ones = A.alloc([128], BF16)
        vecs = A.alloc([NV], F32)
        nsp8 = A.alloc([16], F32)
        nsp16 = A.alloc([16], F32)
        hn = A.alloc([KC, 1152], BF16)
        NSLAB = 2
        slab = [A.alloc([KC * 512], BF16) for _ in range(NSLAB)]
        b_slab = [Buf("slab%d" % i) for i in range(NSLAB)]
        b_hn = [Buf("hn%d" % i) for i in range(9)]
        b_const = Buf("const")
        persist_end = A.off
        slab_rr = [0]

        def next_slab():
            i = slab_rr[0]
            slab_rr[0] = (i + 1) % NSLAB
            return i

        def slab3(i, ncols):
            return slab[i][:, 0:KC * ncols].rearrange("p (k n) -> p k n", k=KC, n=ncols)

        def wload(i, dst, src, extra_reads=()):
            S.dma("gpsimd", lambda e: e.dma_start(out=dst, in_=src), reads=list(extra_reads), writes=[b_slab[i]])

        def wcols(W, c0, n):
            return W[:, c0:c0 + n].rearrange("(k p) n -> p k n", p=128)

        def mm(out, pairs, reads, writes):
            def fn(e):
                n = len(pairs)
                ins = None
                for i, (l, r) in enumerate(pairs):
                    ins = e.matmul(out, l, r, start=(i == 0), stop=(i == n - 1))
                return ins
            return S.op("tensor", fn, reads, writes)

        def V(fn, reads, writes):
            return S.op("vector", fn, reads, writes)

        def ACT(fn, reads, writes):
            return S.op("scalar", fn, reads, writes)

        def POOL(fn, reads, writes):
            return S.op("gpsimd", fn, reads, writes)

        def vcol(base, kc):
            return vecs[:, base + kc: base + kc + 1]


        def checkpoint(k, dumps=()):
            if STOP != k:
                return
            S.barrier()
            for ap2, row0 in dumps:
                n = ap2.shape[1]
                S.dma("gpsimd", lambda e, ap2=ap2, row0=row0, n=n: e.dma_start(out=out_d[row0:row0 + 128, 0:n], in_=ap2),
                      writes=[Buf()])
            S.barrier()
            S.emit()
            raise _Stop()

        S.dma("sync", lambda e: e.dma_start(out=vecs, in_=vecs_d), writes=[b_const])
        S.dma("gpsimd", lambda e: e.dma_start(out=ident, in_=ident_d), writes=[b_const])
        V(lambda e: e.memset(ones, 1.0), [], [b_const])

        checkpoint(-2, [(vecs[:, 0:NV], 0)])
        def rmsnorm_to_fm(tiles, wbase, dst_cols, tmpA):
            n = len(tiles)
            ss, rstd = tmpA["ss"], tmpA["rstd"]
            b_ss = Buf("ss")
            for i, (src, sb, _, _) in enumerate(tiles):
                jb = tmpA["b_junk"][i % 2]
                ACT(lambda e, src=src, i=i: e.activation(out=tmpA["junk"][i % 2], in_=src, func=AF.Square,
                                                         accum_out=ss[:, i:i + 1]),
                    [sb], [jb, b_ss])
            ACT(lambda e: e.activation(out=rstd[:, 0:n], in_=ss[:, 0:n], func=AF.Sqrt, scale=1.0 / D, bias=tmpA["eps"]),
                [b_ss, b_const], [b_ss])
            V(lambda e: e.reciprocal(out=rstd[:, 0:n], in_=rstd[:, 0:n]), [b_ss], [b_ss])
            for i, (src, sb, col0, hb) in enumerate(tiles):
                xs = tmpA["xs"][i % 2]
                xb = tmpA["b_xs"][i % 2]
                ACT(lambda e, src=src, i=i, xs=xs: e.activation(out=xs, in_=src, func=AF.Copy, scale=rstd[:, i:i + 1]),
                    [sb, b_ss], [xb])
                for g4 in range(4):
                    pb = pbuf[2 + g4][0]
                    pv = psb[:, (2 + g4) * 1024:(2 + g4) * 1024 + 512]

                    def tfn(e, xs=xs, g4=g4, pv=pv):
                        ins = None
                        for q in range(4):
                            kc = g4 * 4 + q
                            ins = e.transpose(pv[:, q * 128:(q + 1) * 128], xs[:, kc * 128:(kc + 1) * 128], ident)
                        return ins
                    S.op("tensor", tfn, [xb, b_const], [pb])
                    wv = vecs[:, wbase + g4 * 4: wbase + g4 * 4 + 4].unsqueeze(2).to_broadcast([128, 4, 128])
                    V(lambda e, pv=pv, g4=g4, col0=col0, wv=wv: e.tensor_tensor(
                        out=hn[:, g4 * 4:g4 * 4 + 4, col0:col0 + 128],
                        in0=pv.rearrange("p (a b) -> p a b", a=4, b=128), in1=wv, op=OP.mult),
                      [pb, b_const], [hb])

        epsN = A.alloc([1], F32)
        epsG = A.alloc([1], F32)
        persist_end = A.off
        V(lambda e: e.memset(epsN, 1e-6), [], [b_const])
        V(lambda e: e.memset(epsG, 1e-5), [], [b_const])

        def norm_tmp():
            return {"ss": A.alloc([16], F32), "rstd": A.alloc([16], F32), "eps": epsN,
                    "junk": [A.alloc([D], BF16) for _ in range(2)], "b_junk": [Buf(), Buf()],
                    "xs": [A.alloc([D], BF16) for _ in range(2)], "b_xs": [Buf(), Buf()]}

        cos2 = A.alloc([NT, 128], F32)
        sinm = A.alloc([NT, 128], F32)
        dtab = A.alloc([H, 128], F32)
        qdec = A.alloc([H, 128], F32)
        kdec = A.alloc([H], F32)
        m1_base = A.off
        b_tab = Buf("tab")
        S.dma("sync", lambda e: e.dma_start(out=dtab.rearrange("p a b -> p (a b)"), in_=dt_d), writes=[b_tab])
        S.dma("sync", lambda e: e.dma_start(out=qdec.rearrange("p a b -> p (a b)"), in_=qdec_d), writes=[b_tab])
        S.dma("sync", lambda e: e.dma_start(out=kdec, in_=kdec_d), writes=[b_tab])

        nt = norm_tmp()
        xt = [A.alloc([D], F32) for _ in range(2)]
        b_xt = [Buf(), Buf()]
        posi = A.alloc([NT], I32)
        posf = A.alloc([NT], F32)
        invf = A.alloc([64], F32)
        ang = A.alloc([NT, 2, 64], F32)
        kf = A.alloc([NT, 2, 64], F32)
        ki = A.alloc([NT, 2, 64], I32)
        msk = A.alloc([NT, 2, 64], F32)
        S.dma("sync", lambda e: e.dma_start(out=posi, in_=pos_d), writes=[b_tab])
        S.dma("sync", lambda e: e.dma_start(out=invf, in_=invf_d), writes=[b_tab])
        V(lambda e: e.tensor_copy(out=posf, in_=posi), [b_tab], [b_tab])
        for tt in range(NT):
            V(lambda e, tt=tt: e.tensor_scalar(out=ang[:, tt, 0, :], in0=invf, scalar1=posf[:, tt:tt + 1], scalar2=None,
                                               op0=OP.mult), [b_tab], [b_tab])
        a0 = ang[:, :, 0, :]
        a1 = ang[:, :, 1, :]
        k0 = kf[:, :, 0, :]
        ki0 = ki[:, :, 0, :]
        m0 = msk[:, :, 0, :]
        V(lambda e: e.tensor_scalar(out=k0, in0=a0, scalar1=1.0 / TWO_PI, scalar2=None, op0=OP.mult), [b_tab], [b_tab])
        V(lambda e: e.tensor_copy(out=ki0, in_=k0), [b_tab], [b_tab])
        V(lambda e: e.tensor_copy(out=k0, in_=ki0), [b_tab], [b_tab])
        V(lambda e: e.scalar_tensor_tensor(out=a0, in0=k0, scalar=-C1, in1=a0, op0=OP.mult, op1=OP.add), [b_tab], [b_tab])
        V(lambda e: e.scalar_tensor_tensor(out=a0, in0=k0, scalar=-C2, in1=a0, op0=OP.mult, op1=OP.add), [b_tab], [b_tab])
        V(lambda e: e.tensor_scalar(out=m0, in0=a0, scalar1=np.pi, scalar2=None, op0=OP.is_gt), [b_tab], [b_tab])
        V(lambda e: e.scalar_tensor_tensor(out=a0, in0=m0, scalar=-TWO_PI, in1=a0, op0=OP.mult, op1=OP.add), [b_tab], [b_tab])
        V(lambda e: e.tensor_scalar(out=m0, in0=a0, scalar1=-np.pi, scalar2=None, op0=OP.is_lt), [b_tab], [b_tab])
        V(lambda e: e.scalar_tensor_tensor(out=a0, in0=m0, scalar=TWO_PI, in1=a0, op0=OP.mult, op1=OP.add), [b_tab], [b_tab])
        m1 = msk[:, :, 1, :]
        V(lambda e: e.tensor_scalar(out=a1, in0=a0, scalar1=np.pi / 2, scalar2=None, op0=OP.add), [b_tab], [b_tab])
        V(lambda e: e.tensor_scalar(out=m1, in0=a1, scalar1=np.pi, scalar2=None, op0=OP.is_gt), [b_tab], [b_tab])
        V(lambda e: e.scalar_tensor_tensor(out=a1, in0=m1, scalar=-TWO_PI, in1=a1, op0=OP.mult, op1=OP.add), [b_tab], [b_tab])
        V(lambda e: e.tensor_scalar(out=ang, in0=ang, scalar1=3.14159, scalar2=-3.14159, op0=OP.min, op1=OP.max),
          [b_tab], [b_tab])
        ACT(lambda e: e.activation(out=ang, in_=ang, func=AF.Sin), [b_tab], [b_tab])
        V(lambda e: e.tensor_copy(out=cos2[:, :, 0:64], in_=a1), [b_tab], [b_tab])
        V(lambda e: e.tensor_copy(out=cos2[:, :, 64:128], in_=a1), [b_tab], [b_tab])
        V(lambda e: e.tensor_copy(out=sinm[:, :, 64:128], in_=a0), [b_tab], [b_tab])
        V(lambda e: e.tensor_scalar(out=sinm[:, :, 0:64], in0=a0, scalar1=-1.0, scalar2=None, op0=OP.mult), [b_tab], [b_tab])
        ACT(lambda e: e.activation(out=nsp8, in_=vecs[:, V_LAM:V_LAM + 16], func=AF.Exp, scale=-1.0), [b_const], [b_const])
        ACT(lambda e: e.activation(out=nsp8, in_=nsp8, func=AF.Ln, bias=1.0), [b_const], [b_const])
        V(lambda e: e.tensor_scalar(out=nsp16, in0=nsp8, scalar1=-16.0, scalar2=None, op0=OP.mult), [b_const], [b_const])
        V(lambda e: e.tensor_scalar(out=nsp8, in0=nsp8, scalar1=-8.0, scalar2=None, op0=OP.mult), [b_const], [b_const])

        checkpoint(-3, [(cos2.rearrange('p a b -> p (a b)'), 256), (sinm.rearrange('p a b -> p (a b)'), 384), (nsp8, 0)])
        tiles = []
        for i in range(9):
            src_d = xh_d if i == 0 else x_d[(i - 1) * 128:i * 128, :]
            S.dma("sync", lambda e, i=i, src_d=src_d: e.dma_start(out=xt[i % 2], in_=src_d), writes=[b_xt[i % 2]])
            rmsnorm_to_fm([(xt[i % 2], b_xt[i % 2], i * 128, b_hn[i])], V_NMIX, None,
                          {**nt, "ss": nt["ss"][:, i:i + 1], "rstd": nt["rstd"][:, i:i + 1]})
        S.barrier()
        checkpoint(0, [(hn[:, 0, 0:1152], 0), (hn[:, 15, 0:1152], 128), (cos2.rearrange('p a b -> p (a b)'), 256), (sinm.rearrange('p a b -> p (a b)'), 384)])

        A.off = m1_base
        oloc = A.alloc([NT, D], BF16)
        qts = A.alloc([H, T], BF16)
        qkr = [A.alloc([NT, 256], BF16) for _ in range(2)]
        kd = [A.alloc([NT, 128], BF16) for _ in range(2)]
        vv = [A.alloc([NT, 256], BF16) for _ in range(2)]
        qkT = [A.alloc([NT, 256], BF16) for _ in range(2)]
        tA = [A.alloc([256], F32) for _ in range(2)]
        tB = [A.alloc([256], F32) for _ in range(2)]
        sT = [A.alloc([128], BF16) for _ in range(2)]
        Sst = [A.alloc([256], F32) for _ in range(2)]
        Sbf = [A.alloc([256], BF16) for _ in range(2)]
        b_oloc = [[Buf() for _ in range(H)] for _ in range(NT)]
        b_qts = [[Buf() for _ in range(NT)] for _ in range(H)]
        b_qkr = [[Buf() for _ in range(NT)] for _ in range(2)]
        b_kd = [[Buf() for _ in range(NT)] for _ in range(2)]
        b_vv = [[Buf() for _ in range(NT)] for _ in range(2)]
        b_qkT = [[Buf() for _ in range(NT)] for _ in range(2)]
        b_tA = [Buf(), Buf()]
        b_tB = [Buf(), Buf()]
        b_sT = [Buf(), Buf()]
        b_S = [Buf(), Buf()]
        b_Sbf = [Buf(), Buf()]
        b_gin1 = [Buf() for _ in range(H)]
        slab_of_head = {}

        def m1_load(h):
            i = next_slab()
            slab_of_head[h] = i
            s3 = slab3(i, 512)
            wload(i, s3[:, :, 0:128], wcols(w_in, h * 128, 128))
            wload(i, s3[:, :, 128:256], wcols(w_in, 1024 + h * 128, 128))
            wload(i, s3[:, :, 256:512], wcols(w_in, 2048 + h * 256, 256))

        def m1_proj(h, tt):
            i = slab_of_head[h]
            s3 = slab3(i, 512)
            p = h % 2
            bk = tt % 2
            pb = pbuf[bk][0]
            c0 = 128 + tt * 128
            mm(bank(bk), [(hn[:, kc, c0:c0 + 128], s3[:, kc, :]) for kc in range(KC)],
               [b_hn[tt + 1], b_slab[i]], [pb])
            qk = bank(bk, 256).rearrange("p (a b) -> p a b", a=2, b=128)
            ta = tA[tt % 2].rearrange("p (a b) -> p a b", a=2, b=128)
            tb = tB[tt % 2].rearrange("p (a b) -> p a b", a=2, b=128)
            cb = cos2[:, tt, :].unsqueeze(1).to_broadcast([128, 2, 128])
            V(lambda e: e.tensor_tensor(out=ta, in0=qk, in1=cb, op=OP.mult), [pb, b_tab], [b_tA[tt % 2]])
            V(lambda e: e.tensor_tensor(out=tb[:, :, 0:64], in0=qk[:, :, 64:128],
                                        in1=sinm[:, tt, 0:64].unsqueeze(1).to_broadcast([128, 2, 64]), op=OP.mult),
              [pb, b_tab], [b_tB[tt % 2]])
            V(lambda e: e.tensor_tensor(out=tb[:, :, 64:128], in0=qk[:, :, 0:64],
                                        in1=sinm[:, tt, 64:128].unsqueeze(1).to_broadcast([128, 2, 64]), op=OP.mult),
              [pb, b_tab], [b_tB[tt % 2]])
            ACT(lambda e: e.copy(out=vv[p][:, tt, :], in_=bank(bk, 256, 256)), [pb], [b_vv[p][tt]])
            V(lambda e: e.tensor_tensor(out=qkr[p][:, tt, :], in0=tA[tt % 2], in1=tB[tt % 2], op=OP.add),
              [b_tA[tt % 2], b_tB[tt % 2]], [b_qkr[p][tt]])
            V(lambda e: e.tensor_scalar(out=kd[p][:, tt, :], in0=qkr[p][:, tt, 128:256], scalar1=kdec[:, h:h + 1],
                                        scalar2=None, op0=OP.mult), [b_qkr[p][tt], b_tab], [b_kd[p][tt]])
            sl = tt % 2
            pv = psb[:, (2 + sl) * 1024:(2 + sl) * 1024 + 256]
            ptb = pbuf[2 + sl][0]

            def tfn(e):
                e.transpose(pv[:, 0:128], qkr[p][:, tt, 0:128], ident)
                return e.transpose(pv[:, 128:256], qkr[p][:, tt, 128:256], ident)
            S.op("tensor", tfn, [b_qkr[p][tt], b_const], [ptb])
            ACT(lambda e: e.copy(out=qkT[p][:, tt, :], in_=pv), [ptb], [b_qkT[p][tt]])
            V(lambda e: e.tensor_tensor(out=qts[:, h, tt * 128:(tt + 1) * 128], in0=pv[:, 0:128], in1=qdec[:, h, :],
                                        op=OP.mult), [ptb, b_tab], [b_qts[h][tt]])

        def m1_chunk(h, c):
            p = h % 2
            sl = c % 2
            g128 = gam[h] ** 128
            psc = pbuf[(4, 7)[sl]][0]
            sc = bank((4, 7)[sl], 128)
            mm(sc, [(qkT[p][:, c, 128:256], qkT[p][:, c, 0:128])], [b_qkT[p][c]], [psc])
            V(lambda e: e.tensor_tensor(out=sT[sl], in0=sc, in1=dtab[:, h, :], op=OP.mult), [psc, b_tab], [b_sT[sl]])
            po = pbuf[5][0]
            ob = bank(5, 256)
            pairs = [(sT[sl], vv[p][:, c, :])]
            rds = [b_sT[sl], b_vv[p][c]]
            if c > 0:
                pairs.append((qts[:, h, c * 128:(c + 1) * 128], Sbf[p]))
                rds += [b_qts[h][c], b_Sbf[p]]
            mm(ob, pairs, rds, [po])
            ACT(lambda e: e.copy(out=oloc[:, c, h * 256:(h + 1) * 256], in_=ob), [po], [b_oloc[c][h]])
            pk = pbuf[6][0]
            kb = bank(6, 256)
            mm(kb, [(kd[p][:, c, :], vv[p][:, c, :])], [b_kd[p][c], b_vv[p][c]], [pk])
            if c == 0:
                V(lambda e: e.tensor_copy(out=Sst[p], in_=kb), [pk], [b_S[p]])
            else:
                V(lambda e: e.scalar_tensor_tensor(out=Sst[p], in0=Sst[p], scalar=g128, in1=kb, op0=OP.mult, op1=OP.add),
                  [pk], [b_S[p]])
            if c < NT - 1:
                ACT(lambda e: e.copy(out=Sbf[p], in_=Sst[p]), [b_S[p]], [b_Sbf[p]])
            else:
                S.dma("sync", lambda e: e.dma_start(out=gin1.ap()[h * 128:(h + 1) * 128, :], in_=Sst[p]),
                      reads=[b_S[p]], writes=[b_gin1[h]])

        m1_load(0)
        for h in range(H + 1):
            if h + 1 < H:
                m1_load(h + 1)
            for step in range(NT):
                if h < H:
                    m1_proj(h, step)
                if h >= 1:
                    m1_chunk(h - 1, step)
                if h == 1 and step == 0:
                    checkpoint(21, [(oloc[:, 0, :], 0), (qkT[0].rearrange('p a b -> p (a b)'), 128),
                                    (Sst[0], 256), (sT[0], 384)])
            if h == 0:
                checkpoint(20, [(qkr[0].rearrange('p a b -> p (a b)'), 0), (vv[0].rearrange('p a b -> p (a b)'), 128),
                                (kd[0].rearrange('p a b -> p (a b)'), 256), (qkT[0].rearrange('p a b -> p (a b)'), 384),
                                (qts[:, 0, :], 512)])
            if h == 1:
                checkpoint(22, [(oloc[:, tt, :], tt * 128) for tt in range(NT)])

        checkpoint(1, [(oloc[:, tt, :], tt * 128) for tt in range(NT)])
        b_gout1 = Buf("gout1")
        POOL(lambda e: e.collective_compute("AllGather", OP.bypass, replica_groups=RG,
                                            ins=[gin1.ap()], outs=[gout1.ap()]), b_gin1, [b_gout1])
        S.barrier()
        B = m1_base
        KB = 1024

        A.off = B + 48 * KB
        sinbf = A.alloc([H, 256], BF16)
        coef = A.alloc([32], F32)
        m2_after = A.off
        tall = A.alloc([4, H, 256], F32)
        sacc = A.alloc([H, 256], F32)
        b_sin = Buf("sin")
        S.dma("sync", lambda e: e.dma_start(out=coef, in_=coef_d), writes=[b_sin])
        g1v = gout1.ap().rearrange("(r h d) e -> d r h e", r=4, h=H, d=128)
        for r in range(4):
            S.dma("sync", lambda e, r=r: e.dma_start(out=tall[:, r], in_=g1v[:, r]), reads=[b_gout1], writes=[b_sin])
        for r in range(4):
            cb = coef[:, r * 8:(r + 1) * 8].unsqueeze(2).to_broadcast([128, H, 256])
            if r == 0:
                V(lambda e, cb=cb: e.tensor_tensor(out=sacc, in0=tall[:, 0], in1=cb, op=OP.mult), [b_sin], [b_sin])
            else:
                V(lambda e, cb=cb, r=r: e.tensor_tensor(out=tall[:, r], in0=tall[:, r], in1=cb, op=OP.mult), [b_sin], [b_sin])
                V(lambda e, r=r: e.tensor_tensor(out=sacc, in0=sacc, in1=tall[:, r], op=OP.add), [b_sin], [b_sin])
        V(lambda e: e.tensor_copy(out=sinbf, in_=sacc), [b_sin], [b_sin])
        S.barrier()
        checkpoint(2, [(sacc.rearrange('p a b -> p (a b)'), 0)])

        A.off = m2_after
        ret_off = (A.off + 63) // 64 * 64
        retout = A.alloc([KC, T], BF16)
        osum = [A.alloc([NT, 256], F32) for _ in range(2)]
        sg = [A.alloc([NT, 256], BF16) for _ in range(2)]
        bst = A.alloc([NT, 6], F32)
        mv = A.alloc([NT, 2], F32)
        rsd = A.alloc([NT], F32)
        ynorm = [A.alloc([256], F32) for _ in range(2)]
        rtm = [A.alloc([256], BF16) for _ in range(2)]
        b_ret = [Buf() for _ in range(NT)]
        b_osum = [[Buf() for _ in range(NT)] for _ in range(2)]
        b_sg = [[Buf() for _ in range(NT)] for _ in range(2)]
        b_st = Buf()
        b_yn = [Buf(), Buf()]
        b_rtm = [Buf(), Buf()]
        slab_of_pair = {}

        def m2_load(hp):
            i = next_slab()
            slab_of_pair[hp] = i
            wload(i, slab3(i, 512), wcols(w_in, 4096 + hp * 512, 512))

        m2_load(0)
        for h in range(H):
            if h % 2 == 1 and h // 2 + 1 < 4:
                m2_load(h // 2 + 1)
            i = slab_of_pair[h // 2]
            s3 = slab3(i, 512)
            p = h % 2
            for tt in range(NT):
                bk = tt % 2
                c0 = 128 + tt * 128
                pg = pbuf[bk][0]
                gb = bank(bk, 256)
                mm(gb, [(hn[:, kc, c0:c0 + 128], s3[:, kc, p * 256:(p + 1) * 256]) for kc in range(KC)],
                   [b_hn[tt + 1], b_slab[i]], [pg])
                ACT(lambda e, gb=gb, tt=tt, p=p: e.activation(out=sg[p][:, tt, :], in_=gb, func=AF.Silu), [pg], [b_sg[p][tt]])
                pc = pbuf[4 + bk][0]
                cbk = bank(4 + bk, 256)
                mm(cbk, [(qts[:, h, tt * 128:(tt + 1) * 128], sinbf[:, h, :])], [b_sin], [pc])
                V(lambda e, cbk=cbk, tt=tt, p=p, h=h: e.scalar_tensor_tensor(
                    out=osum[p][:, tt, :], in0=cbk, scalar=float(gam[h] ** (128 * tt)),
                    in1=oloc[:, tt, h * 256:(h + 1) * 256], op0=OP.mult, op1=OP.add), [pc], [b_osum[p][tt]])
                V(lambda e, tt=tt, p=p: e.bn_stats(out=bst[:, tt, :], in_=osum[p][:, tt, :]), [b_osum[p][tt]], [b_st])
                V(lambda e, tt=tt: e.bn_aggr(out=mv[:, tt, :], in_=bst[:, tt, :]), [b_st], [b_st])
            ACT(lambda e: e.activation(out=rsd, in_=mv[:, :, 1], func=AF.Sqrt, bias=epsG), [b_st, b_const], [b_st])
            V(lambda e: e.reciprocal(out=rsd, in_=rsd), [b_st], [b_st])
            for tt in range(NT):
                q = tt % 2
                V(lambda e, tt=tt, q=q, p=p: e.tensor_scalar(out=ynorm[q], in0=osum[p][:, tt, :], scalar1=mv[:, tt, 0:1],
                                                             scalar2=rsd[:, tt:tt + 1], op0=OP.subtract, op1=OP.mult),
                  [b_osum[p][tt], b_st], [b_yn[q]])
                V(lambda e, tt=tt, q=q, p=p: e.tensor_tensor(out=rtm[q], in0=ynorm[q], in1=sg[p][:, tt, :], op=OP.mult),
                  [b_yn[q], b_sg[p][tt]], [b_rtm[q]])
                pv = psb[:, (2 + q) * 1024:(2 + q) * 1024 + 256]
                ptb = pbuf[2 + q][0]

                def tfn(e, q=q, pv=pv):
                    e.transpose(pv[:, 0:128], rtm[q][:, 0:128], ident)
                    return e.transpose(pv[:, 128:256], rtm[q][:, 128:256], ident)
                S.op("tensor", tfn, [b_rtm[q], b_const], [ptb])
                for ec in range(2):
                    kc = h * 2 + ec
                    ACT(lambda e, pv=pv, ec=ec, kc=kc, tt=tt: e.activation(
                        out=retout[:, kc, tt * 128:(tt + 1) * 128], in_=pv[:, ec * 128:(ec + 1) * 128],
                        func=AF.Copy, scale=vcol(V_GN, kc)), [ptb, b_const], [b_ret[tt]])
        S.barrier()
        checkpoint(3, [(retout[:, kc, :], kc * 128) for kc in range(4)] + [(retout[:, 15, :], 512)])

        A.off = B
        mg = A.alloc([KC, T], BF16)
        b_mg = [[Buf() for _ in range(2)] for _ in range(KC)]
        hn_half = [[b_hn[1 + th * 4 + q] for q in range(4)] for th in range(2)]

        def branch_gate(src, Wp, gcol0, accumulate, sgt, tmpm):
            b_sgt = [Buf(), Buf()]
            b_tmpm = [Buf(), Buf()]

            def load(sp):
                i = next_slab()
                s3 = slab3(i, 512)
                wload(i, s3[:, :, 0:256], wcols(Wp, sp * 256, 256))
                wload(i, s3[:, :, 256:512], wcols(w_in, gcol0 + sp * 256, 256))
                return i
            cur = load(0)
            for sp in range(8):
                nxt = load(sp + 1) if sp + 1 < 8 else None
                s3 = slab3(cur, 512)
                for oq in range(2):
                    oc = sp * 2 + oq
                    for th in range(2):
                        k = th
                        p1 = pbuf[k * 2][0]
                        p2 = pbuf[k * 2 + 1][0]
                        tk = slice(th * 512, (th + 1) * 512)
                        hk = slice(128 + th * 512, 128 + (th + 1) * 512)
                        mm(bank(k * 2), [(s3[:, kc, oq * 128:(oq + 1) * 128], src[:, kc, tk]) for kc in range(KC)],
                           [b_slab[cur]], [p1])
                        mm(bank(k * 2 + 1), [(s3[:, kc, 256 + oq * 128:256 + (oq + 1) * 128], hn[:, kc, hk])
                                             for kc in range(KC)], [b_slab[cur]] + hn_half[th], [p2])
                        ACT(lambda e, k=k: e.activation(out=sgt[k], in_=bank(k * 2 + 1), func=AF.Sigmoid), [p2], [b_sgt[k]])
                        if not accumulate:
                            V(lambda e, k=k, oc=oc, tk=tk: e.tensor_tensor(out=mg[:, oc, tk], in0=bank(k * 2), in1=sgt[k],
                                                                           op=OP.mult), [p1, b_sgt[k]], [b_mg[oc][th]])
                        else:
                            V(lambda e, k=k: e.tensor_tensor(out=tmpm[k], in0=bank(k * 2), in1=sgt[k], op=OP.mult),
                              [p1, b_sgt[k]], [b_tmpm[k]])
                            V(lambda e, k=k, oc=oc, tk=tk: e.tensor_tensor(out=mg[:, oc, tk], in0=tmpm[k], in1=mg[:, oc, tk],
                                                                           op=OP.add), [b_tmpm[k]], [b_mg[oc][th]])
                cur = nxt

        sgt3 = [A.alloc([512], F32) for _ in range(2)]
        tmp3 = [A.alloc([512], F32) for _ in range(2)]
        assert A.off <= ret_off
        branch_gate(retout, w_pret, 10240, False, sgt3, tmp3)
        S.barrier()
        checkpoint(4, [(mg[:, kc, :], kc * 128) for kc in range(8)])

        hs_d = nc.dram_tensor("hs_d", [KC, 128, T], F32)
        ac_d = nc.dram_tensor("ac_d", [KC, 128, T], F32)
        A.off = B + 32 * KB
        wga = A.alloc([16, 128], BF16)
        wgx = A.alloc([16, 128], BF16)
        xc = A.alloc([T], F32)
        xcb = A.alloc([T], BF16)
        ra = A.alloc([T], F32)
        a2 = A.alloc([T], F32)
        iu = A.alloc([T], F32)
        hsc = [A.alloc([T], F32) for _ in range(2)]
        asc = [A.alloc([T], F32) for _ in range(2)]
        zer = A.alloc([T], BF16)
        ext = A.alloc([1032], F32)
        ends = A.alloc([32], F32)
        b_wg = Buf()
        b_xc, b_xcb, b_ra, b_a2, b_iu, b_zer, b_ext, b_ends = (Buf() for _ in range(8))
        b_hsc = [Buf(), Buf()]
        b_asc = [Buf(), Buf()]
        b_hsd = [Buf() for _ in range(KC)]
        b_acd = [Buf() for _ in range(KC)]
        S.dma("gpsimd", lambda e: e.dma_start(out=wga, in_=rg_wa.rearrange("g i j -> i g j")), writes=[b_wg])
        S.dma("gpsimd", lambda e: e.dma_start(out=wgx, in_=rg_wx.rearrange("g i j -> i g j")), writes=[b_wg])
        V(lambda e: e.memset(zer, 0.0), [], [b_zer])
        hn_all = [b_hn[i] for i in range(9)]
        cur = None
        for g in range(KC):
            if g % 4 == 0:
                cur = next_slab()
                wload(cur, slab3(cur, 512), wcols(w_in, 6144 + g * 128, 512))
            s3 = slab3(cur, 512)
            wsl = slice((g % 4) * 128, (g % 4 + 1) * 128)
            q = g % 2
            pX = [pbuf[0][0], pbuf[1][0], pbuf[2][0]]
            mm(bank(0), [(s3[:, kc, wsl], hn[:, kc, 125:637]) for kc in range(KC)], [b_slab[cur]] + hn_all, [pX[0]])
            mm(bank(1), [(s3[:, kc, wsl], hn[:, kc, 637:1149]) for kc in range(KC)], [b_slab[cur]] + hn_all, [pX[1]])
            mm(bank(2, 3), [(s3[:, kc, wsl], hn[:, kc, 1149:1152]) for kc in range(KC)], [b_slab[cur]] + hn_all, [pX[2]])
            ACT(lambda e: e.copy(out=ext[:, 0:1024], in_=ps[:, 0:1024]), [pX[0], pX[1]], [b_ext])
            ACT(lambda e: e.copy(out=ext[:, 1024:1027], in_=bank(2, 3)), [pX[2]], [b_ext])
            V(lambda e, g=g: e.tensor_scalar(out=xc, in0=ext[:, 0:1024], scalar1=vcol(V_CW, g), scalar2=vcol(V_CB, g),
                                             op0=OP.mult, op1=OP.add), [b_ext, b_const], [b_xc])
            for j in range(1, 4):
                V(lambda e, g=g, j=j: e.scalar_tensor_tensor(out=xc, in0=ext[:, j:j + 1024], scalar=vcol(V_CW + 16 * j, g),
                                                             in1=xc, op0=OP.mult, op1=OP.add), [b_ext, b_const], [b_xc])
            ACT(lambda e: e.copy(out=xcb, in_=xc), [b_xc], [b_xcb])
            pG = [pbuf[3][0], pbuf[4][0], pbuf[5][0], pbuf[6][0]]
            for th in range(2):
                mm(bank(3 + th), [(wga[:, g, :], xcb[:, th * 512:(th + 1) * 512])], [b_wg, b_xcb], [pG[th]])
                mm(bank(5 + th), [(wgx[:, g, :], xcb[:, th * 512:(th + 1) * 512])], [b_wg, b_xcb], [pG[2 + th]])
            ACT(lambda e, g=g: e.activation(out=ra, in_=ps[:, 3 * 512:5 * 512], func=AF.Sigmoid, bias=vcol(V_BA, g)),
                [pG[0], pG[1], b_const], [b_ra])
            ACT(lambda e, g=g: e.activation(out=iu, in_=ps[:, 5 * 512:7 * 512], func=AF.Sigmoid, bias=vcol(V_BX, g)),
                [pG[2], pG[3], b_const], [b_iu])
            ACT(lambda e, g=g: e.activation(out=a2, in_=ra, func=AF.Exp, scale=nsp16[:, g:g + 1]), [b_ra, b_const], [b_a2])
            ACT(lambda e, g=g: e.activation(out=ra, in_=ra, func=AF.Exp, scale=nsp8[:, g:g + 1]), [b_ra, b_const], [b_ra])
            ACT(lambda e: e.activation(out=a2, in_=a2, func=AF.Sqrt, scale=-1.0, bias=1.0), [b_a2], [b_a2])
            V(lambda e: e.tensor_tensor(out=iu, in0=iu, in1=xc, op=OP.mult), [b_iu, b_xc], [b_iu])
            V(lambda e: e.tensor_tensor(out=iu, in0=iu, in1=a2, op=OP.mult), [b_iu, b_a2], [b_iu])
            V(lambda e, q=q: e.tensor_tensor_scan(out=hsc[q], data0=ra, data1=iu, initial=0.0, op0=OP.mult, op1=OP.add),
              [b_ra, b_iu], [b_hsc[q]])
            V(lambda e, g=g, q=q: e.tensor_copy(out=ends[:, 16 + g:17 + g], in_=hsc[q][:, T - 1:T]), [b_hsc[q]], [b_ends])
            S.dma("sync", lambda e, g=g, q=q: e.dma_start(out=hs_d.ap()[g], in_=hsc[q]), reads=[b_hsc[q]], writes=[b_hsd[g]])
            V(lambda e, q=q: e.tensor_tensor_scan(out=asc[q], data0=ra, data1=zer, initial=1.0, op0=OP.mult, op1=OP.add),
              [b_ra, b_zer], [b_asc[q]])
            V(lambda e, g=g, q=q: e.tensor_copy(out=ends[:, g:g + 1], in_=asc[q][:, T - 1:T]), [b_asc[q]], [b_ends])
            S.dma("sync", lambda e, g=g, q=q: e.dma_start(out=ac_d.ap()[g], in_=asc[q]), reads=[b_asc[q]], writes=[b_acd[g]])
        b_gin2, b_gout2 = Buf(), Buf()
        S.dma("sync", lambda e: e.dma_start(out=gin2.ap(), in_=ends), reads=[b_ends], writes=[b_gin2])
        POOL(lambda e: e.collective_compute("AllGather", OP.bypass, replica_groups=RG,
                                            ins=[gin2.ap()], outs=[gout2.ap()]), [b_gin2], [b_gout2])
        S.barrier()
        checkpoint(5, [(ends, 0)])

        A.off = B + 32 * KB
        rnnout = A.alloc([KC, T], BF16)
        g2 = A.alloc([4, 32], F32)
        mskl = A.alloc([4], F32)
        hin = A.alloc([16], F32)
        tmpl = A.alloc([16], F32)
        hl = [A.alloc([T], F32) for _ in range(2)]
        al = [A.alloc([T], F32) for _ in range(2)]
        t1 = A.alloc([T], F32)
        gy = A.alloc([T], F32)
        b_l = Buf()
        b_hl = [Buf(), Buf()]
        b_al = [Buf(), Buf()]
        b_t1, b_gy = Buf(), Buf()
        b_rnn = [Buf() for _ in range(KC)]
        S.dma("sync", lambda e: e.dma_start(out=g2, in_=gout2.ap().rearrange("(r p) c -> p r c", r=4)), reads=[b_gout2], writes=[b_l])
        S.dma("sync", lambda e: e.dma_start(out=mskl, in_=mskl_d), writes=[b_l])
        V(lambda e: e.memset(hin, 0.0), [], [b_l])
        for r in range(4):
            V(lambda e, r=r: e.tensor_tensor(out=tmpl, in0=g2[:, r, 0:16], in1=hin, op=OP.mult), [b_l], [b_l])
            V(lambda e, r=r: e.tensor_tensor(out=tmpl, in0=tmpl, in1=g2[:, r, 16:32], op=OP.add), [b_l], [b_l])
            V(lambda e: e.tensor_tensor(out=tmpl, in0=tmpl, in1=hin, op=OP.subtract), [b_l], [b_l])
            V(lambda e, r=r: e.scalar_tensor_tensor(out=hin, in0=tmpl, scalar=mskl[:, r:r + 1], in1=hin, op0=OP.mult, op1=OP.add),
              [b_l], [b_l])
        cur = None
        for g in range(KC):
            if g % 4 == 0:
                cur = next_slab()
                wload(cur, slab3(cur, 512), wcols(w_in, 8192 + g * 128, 512))
            s3 = slab3(cur, 512)
            wsl = slice((g % 4) * 128, (g % 4 + 1) * 128)
            q = g % 2
            S.dma("sync", lambda e, g=g, q=q: e.dma_start(out=hl[q], in_=hs_d.ap()[g]), reads=[b_hsd[g]], writes=[b_hl[q]])
            S.dma("sync", lambda e, g=g, q=q: e.dma_start(out=al[q], in_=ac_d.ap()[g]), reads=[b_acd[g]], writes=[b_al[q]])
            pY = [pbuf[q * 2][0], pbuf[q * 2 + 1][0]]
            for th in range(2):
                mm(bank(q * 2 + th), [(s3[:, kc, wsl], hn[:, kc, 128 + th * 512:128 + (th + 1) * 512]) for kc in range(KC)],
                   [b_slab[cur]] + hn_half[th], [pY[th]])
            yv = ps[:, q * 1024:(q + 1) * 1024]
            ACT(lambda e, yv=yv: e.activation(out=t1, in_=yv, func=AF.Square), pY, [b_t1])
            V(lambda e: e.tensor_scalar(out=t1, in0=t1, scalar1=0.044715, scalar2=1.0, op0=OP.mult, op1=OP.add), [b_t1], [b_t1])
            V(lambda e, yv=yv: e.tensor_tensor(out=t1, in0=yv, in1=t1, op=OP.mult), pY + [b_t1], [b_t1])
            ACT(lambda e: e.activation(out=t1, in_=t1, func=AF.Sigmoid, scale=1.5957691216057308), [b_t1], [b_t1])
            V(lambda e, yv=yv: e.tensor_tensor(out=gy, in0=yv, in1=t1, op=OP.mult), pY + [b_t1], [b_gy])
            V(lambda e, g=g, q=q: e.scalar_tensor_tensor(out=hl[q], in0=al[q], scalar=hin[:, g:g + 1], in1=hl[q],
                                                         op0=OP.mult, op1=OP.add), [b_al[q], b_hl[q], b_l], [b_hl[q]])
            V(lambda e, g=g, q=q: e.tensor_tensor(out=rnnout[:, g, :], in0=gy, in1=hl[q], op=OP.mult),
              [b_gy, b_hl[q]], [b_rnn[g]])
        S.barrier()
        checkpoint(6, [(rnnout[:, kc, :], kc * 128) for kc in range(8)])

        A.off = B + 64 * KB
        sgt6 = [A.alloc([512], F32) for _ in range(2)]
        tmp6 = [A.alloc([512], F32) for _ in range(2)]
        branch_gate(rnnout, w_prnn, 12288, True, sgt6, tmp6)
        S.barrier()
        checkpoint(7, [(mg[:, kc, :], kc * 128) for kc in range(8)])

        A.off = B + 40 * KB
        hres = A.alloc([NT, D], F32)
        h_top = A.off
        b_h = [[Buf() for _ in range(4)] for _ in range(NT)]
        for tt in range(NT):
            S.dma("sync", lambda e, tt=tt: e.dma_start(out=hres[:, tt, :], in_=x_d[tt * 128:(tt + 1) * 128, :]),
                  writes=b_h[tt])

        def tm_proj_acc(act, nk, wsrc_fn, nslabs):
            def load(cs):
                i = next_slab()
                dst = slab[i][:, 0:nk * 512].rearrange("p (k n) -> p k n", k=nk, n=512)
                wload(i, dst, wsrc_fn(cs))
                return i, dst
            cur = load(0)
            for cs in range(nslabs):
                nxt = load(cs + 1) if cs + 1 < nslabs else None
                i, w3 = cur
                for tt in range(NT):
                    k = 4 + (tt % 2)
                    pb = pbuf[k][0]
                    mm(bank(k), [(act[:, kk, tt * 128:(tt + 1) * 128], w3[:, kk, :]) for kk in range(nk)], [b_slab[i]], [pb])
                    V(lambda e, k=k, tt=tt, cs=cs: e.tensor_tensor(out=hres[:, tt, cs * 512:(cs + 1) * 512], in0=bank(k),
                                                                   in1=hres[:, tt, cs * 512:(cs + 1) * 512], op=OP.add),
                      [pb], [b_h[tt][cs]])
                cur = nxt

        tm_proj_acc(mg, KC, lambda cs: wcols(w_mix, cs * 512, 512), 4)
        S.barrier()
        checkpoint(8, [(hres[:, tt, :], tt * 128) for tt in range(NT)])

        def h_norm(wbase):
            A.off = B
            ntx = norm_tmp()
            assert A.off <= B + 40 * KB
            tiles_ = [(hres[:, tt, :], None, 128 + tt * 128, b_hn[tt + 1]) for tt in range(NT)]
            for i, (src, _, col0, hb) in enumerate(tiles_):
                rmsnorm_to_fm([(src, Buf(), col0, hb)], wbase, None,
                              {**ntx, "ss": ntx["ss"][:, i:i + 1], "rstd": ntx["rstd"][:, i:i + 1]})
            return ntx

        ntx = h_norm(V_NXA)
        memst = A.alloc([2, D], F32)
        assert A.off <= B + 40 * KB
        A.off = h_top
        memn = A.alloc([KC, 256], BF16)
        b_memst = Buf()
        b_memn = Buf()
        S.dma("sync", lambda e: e.dma_start(out=memst, in_=mem_d.rearrange("(a p) d -> p a d", p=128)), writes=[b_memst])
        for mt in range(2):
            src = memst[:, mt, :]
            ssv = ntx["ss"][:, 8 + mt:9 + mt]
            rsv = ntx["rstd"][:, 8 + mt:9 + mt]
            jb = ntx["b_junk"][mt]
            xb = ntx["b_xs"][mt]
            xs = ntx["xs"][mt]
            b_ss = Buf()
            ACT(lambda e, src=src, mt=mt, ssv=ssv: e.activation(out=ntx["junk"][mt], in_=src, func=AF.Square, accum_out=ssv),
                [b_memst], [jb, b_ss])
            ACT(lambda e, ssv=ssv, rsv=rsv: e.activation(out=rsv, in_=ssv, func=AF.Sqrt, scale=1.0 / D, bias=epsN),
                [b_ss, b_const], [b_ss])
            V(lambda e, rsv=rsv: e.reciprocal(out=rsv, in_=rsv), [b_ss], [b_ss])
            ACT(lambda e, src=src, xs=xs, rsv=rsv: e.activation(out=xs, in_=src, func=AF.Copy, scale=rsv), [b_memst, b_ss], [xb])
            for g4 in range(4):
                pb = pbuf[2 + g4][0]
                pv = psb[:, (2 + g4) * 1024:(2 + g4) * 1024 + 512]

                def tfn(e, xs=xs, g4=g4, pv=pv):
                    ins = None
                    for qq in range(4):
                        kc = g4 * 4 + qq
                        ins = e.transpose(pv[:, qq * 128:(qq + 1) * 128], xs[:, kc * 128:(kc + 1) * 128], ident)
                    return ins
                S.op("tensor", tfn, [xb, b_const], [pb])
                wv = vecs[:, V_NMEM + g4 * 4: V_NMEM + g4 * 4 + 4].unsqueeze(2).to_broadcast([128, 4, 128])
                V(lambda e, pv=pv, g4=g4, mt=mt, wv=wv: e.tensor_tensor(
                    out=memn[:, g4 * 4:g4 * 4 + 4, mt * 128:(mt + 1) * 128],
                    in0=pv.rearrange("p (a b) -> p a b", a=4, b=128), in1=wv, op=OP.mult), [pb, b_const], [b_memn])
        S.barrier()
        checkpoint(9, [(memn.rearrange('p a b -> p (a b)')[:, 0:2048], 0), (hn[:, 0, 0:1152], 128)])

        A.off = B
        kth = A.alloc([4, 256], BF16)
        vh = A.alloc([2, 512], BF16)
        qT = A.alloc([4, T], BF16)
        oT = A.alloc([4, T], BF16)
        pT = [[A.alloc([512], BF16) for _ in range(2)] for _ in range(2)]
        rs = [A.alloc([512], F32) for _ in range(2)]
        assert A.off <= B + 40 * KB
        b_kth, b_vh = Buf(), Buf()
        b_qT = [[Buf() for _ in range(2)] for _ in range(4)]
        b_oT = [[Buf() for _ in range(2)] for _ in range(4)]
        b_pT = [[Buf() for _ in range(2)] for _ in range(2)]
        b_rs = [Buf(), Buf()]
        SCALE = 512.0 ** -0.5
        for hd in range(4):
            ik = next_slab()
            wload(ik, slab3(ik, 512), wcols(w_xk, hd * 512, 512))
            sk = slab3(ik, 512)
            for dc in range(4):
                pb = pbuf[dc % 2][0]
                mm(bank(dc % 2, 256), [(sk[:, kc, dc * 128:(dc + 1) * 128], memn[:, kc, :]) for kc in range(KC)],
                   [b_slab[ik], b_memn], [pb])
                ACT(lambda e, dc=dc: e.copy(out=kth[:, dc, :], in_=bank(dc % 2, 256)), [pb], [b_kth])
            iv = next_slab()
            wload(iv, slab3(iv, 512), wcols(w_xv, hd * 512, 512))
            sv = slab3(iv, 512)
            for mt in range(2):
                pb = pbuf[2 + mt][0]
                mm(bank(2 + mt), [(memn[:, kc, mt * 128:(mt + 1) * 128], sv[:, kc, :]) for kc in range(KC)],
                   [b_slab[iv], b_memn], [pb])
                ACT(lambda e, mt=mt: e.copy(out=vh[:, mt, :], in_=bank(2 + mt)), [pb], [b_vh])
            iq = next_slab()
            wload(iq, slab3(iq, 512), wcols(w_xq, hd * 512, 512))
            sq = slab3(iq, 512)
            for dc in range(4):
                for th in range(2):
                    k = 4 + (dc * 2 + th) % 2
                    pb = pbuf[k][0]
                    mm(bank(k), [(sq[:, kc, dc * 128:(dc + 1) * 128], hn[:, kc, 128 + th * 512:128 + (th + 1) * 512])
                                 for kc in range(KC)], [b_slab[iq]] + hn_half[th], [pb])
                    ACT(lambda e, k=k, dc=dc, th=th: e.copy(out=qT[:, dc, th * 512:(th + 1) * 512], in_=bank(k)),
                        [pb], [b_qT[dc][th]])
            for th in range(2):
                tk = slice(th * 512, (th + 1) * 512)
                for mt in range(2):
                    pb = pbuf[mt][0]
                    mm(bank(mt), [(kth[:, dc, mt * 128:(mt + 1) * 128], qT[:, dc, tk]) for dc in range(4)],
                       [b_kth] + [b_qT[dc][th] for dc in range(4)], [pb])
                    ACT(lambda e, mt=mt, th=th: e.activation(out=pT[th][mt], in_=bank(mt), func=AF.Exp, scale=SCALE),
                        [pb], [b_pT[th][mt]])
                pbs = pbuf[2][0]
                mm(bank(2), [(ones, pT[th][mt]) for mt in range(2)], [b_const, b_pT[th][0], b_pT[th][1]], [pbs])
                V(lambda e, th=th: e.reciprocal(out=rs[th], in_=bank(2)), [pbs], [b_rs[th]])
                for ec in range(4):
                    k = 4 + ec % 2
                    pb = pbuf[k][0]
                    mm(bank(k), [(vh[:, mt, ec * 128:(ec + 1) * 128], pT[th][mt]) for mt in range(2)],
                       [b_vh, b_pT[th][0], b_pT[th][1]], [pb])
                    V(lambda e, k=k, ec=ec, tk=tk, th=th: e.tensor_tensor(out=oT[:, ec, tk], in0=bank(k), in1=rs[th], op=OP.mult),
                      [pb, b_rs[th]], [b_oT[ec][th]])
            io = next_slab()
            wo3 = slab[io][:, 0:4 * 2048].rearrange("p (k n) -> p k n", k=4, n=2048)
            wload(io, wo3, w_xo[hd * 512:(hd + 1) * 512, :].rearrange("(k p) n -> p k n", p=128))
            for cs in range(4):
                for tt in range(NT):
                    k = 6 + (tt % 2)
                    pb = pbuf[k][0]
                    mm(bank(k), [(oT[:, kk, tt * 128:(tt + 1) * 128], wo3[:, kk, cs * 512:(cs + 1) * 512]) for kk in range(4)],
                       [b_slab[io]] + [b_oT[kk][tt // 4] for kk in range(4)], [pb])
                    V(lambda e, k=k, tt=tt, cs=cs: e.tensor_tensor(out=hres[:, tt, cs * 512:(cs + 1) * 512], in0=bank(k),
                                                                   in1=hres[:, tt, cs * 512:(cs + 1) * 512], op=OP.add),
                      [pb], [b_h[tt][cs]])
        S.barrier()
        checkpoint(10, [(hres[:, tt, :], tt * 128) for tt in range(NT)])

        h_norm(V_NFFN)
        A.off = h_top
        hal = A.alloc([32], F32)
        g3 = A.alloc([4, 32], F32)
        mskp = A.alloc([4], F32)
        hacc = A.alloc([32], F32)
        b_hal, b_gin3, b_gout3, b_g3 = Buf(), Buf(), Buf(), Buf()
        V(lambda e: e.tensor_copy(out=hal.rearrange("p (a b) -> p a b", a=16, b=2), in_=hn[:, :, 1150:1152]), [b_hn[8]], [b_hal])
        S.dma("sync", lambda e: e.dma_start(out=gin3.ap(), in_=hal), reads=[b_hal], writes=[b_gin3])
        POOL(lambda e: e.collective_compute("AllGather", OP.bypass, replica_groups=RG,
                                            ins=[gin3.ap()], outs=[gout3.ap()]), [b_gin3], [b_gout3])
        S.dma("sync", lambda e: e.dma_start(out=g3, in_=gout3.ap().rearrange("(r p) c -> p r c", r=4)), reads=[b_gout3], writes=[b_g3])
        S.dma("sync", lambda e: e.dma_start(out=mskp, in_=mskp_d), writes=[b_g3])
        V(lambda e: e.tensor_scalar(out=hacc, in0=g3[:, 0, :], scalar1=mskp[:, 0:1], scalar2=None, op0=OP.mult), [b_g3], [b_g3])
        for r in range(1, 4):
            V(lambda e, r=r: e.scalar_tensor_tensor(out=hacc, in0=g3[:, r, :], scalar=mskp[:, r:r + 1], in1=hacc,
                                                    op0=OP.mult, op1=OP.add), [b_g3], [b_g3])
        V(lambda e: e.tensor_copy(out=hn[:, :, 126:128], in_=hacc.rearrange("p (a b) -> p a b", a=16, b=2)), [b_g3], [b_hn[0]])
        S.barrier()
        checkpoint(11, [(hn[:, 0, 0:1152], 0), (hn[:, 15, 0:1152], 128)])

        A.off = B
        act = [A.alloc([4, T], BF16) for _ in range(2)]
        gext = A.alloc([1032], F32)
        vext = A.alloc([1032], F32)
        gcv = A.alloc([T], F32)
        vcv = A.alloc([T], F32)
        assert A.off <= B + 40 * KB
        b_act = [[Buf() for _ in range(4)] for _ in range(2)]
        b_gext, b_vext, b_gcv, b_vcv = Buf(), Buf(), Buf(), Buf()
        hn_all = [b_hn[i] for i in range(9)]
        cur = None
        for c in range(NFC):
            if c % 2 == 0:
                cur = next_slab()
                s3 = slab3(cur, 512)
                for q2 in range(2):
                    wload(cur, s3[:, :, q2 * 256:q2 * 256 + 128], wcols(w_up, (c + q2) * 128, 128))
                    wload(cur, s3[:, :, q2 * 256 + 128:q2 * 256 + 256], wcols(w_up, DFF + (c + q2) * 128, 128))
            s3 = slab3(cur, 512)
            grp, ci = divmod(c, 4)
            ap_ = grp % 2
            for (u, extb, b_e, pbase, hoff) in ((0, gext, b_gext, 0, 0), (1, vext, b_vext, 2, 0)):
                wsl = slice((c % 2) * 256 + u * 128, (c % 2) * 256 + u * 128 + 128)
                pA, pBk, pC = pbuf[pbase][0], pbuf[pbase + 1][0], pbuf[6 + u][0]
                mm(bank(pbase), [(s3[:, kc, wsl], hn[:, kc, 126:638]) for kc in range(KC)], [b_slab[cur]] + hn_all, [pA])
                mm(bank(pbase + 1), [(s3[:, kc, wsl], hn[:, kc, 638:1150]) for kc in range(KC)], [b_slab[cur]] + hn_all, [pBk])
                mm(bank(6 + u, 2, hoff), [(s3[:, kc, wsl], hn[:, kc, 1150:1152]) for kc in range(KC)], [b_slab[cur]] + hn_all, [pC])
                ACT(lambda e, extb=extb, pbase=pbase: e.copy(out=extb[:, 0:1024], in_=ps[:, pbase * 512:pbase * 512 + 1024]),
                    [pA, pBk], [b_e])
                ACT(lambda e, extb=extb, hoff=hoff, u=u: e.copy(out=extb[:, 1024:1026], in_=bank(6 + u, 2, hoff)), [pC], [b_e])
            for (u, extb, b_e, cv, b_cv) in ((0, gext, b_gext, gcv, b_gcv), (1, vext, b_vext, vcv, b_vcv)):
                col = u * NFC + c
                V(lambda e, extb=extb, cv=cv, col=col: e.tensor_scalar(
                    out=cv, in0=extb[:, 0:1024], scalar1=vecs[:, V_FCW + col:V_FCW + col + 1],
                    scalar2=vecs[:, V_FCB + col:V_FCB + col + 1], op0=OP.mult, op1=OP.add), [b_e, b_const], [b_cv])
                for j in range(1, 3):
                    V(lambda e, extb=extb, cv=cv, col=col, j=j: e.scalar_tensor_tensor(
                        out=cv, in0=extb[:, j:j + 1024], scalar=vecs[:, V_FCW + 88 * j + col:V_FCW + 88 * j + col + 1],
                        in1=cv, op0=OP.mult, op1=OP.add), [b_e, b_const], [b_cv])
            ACT(lambda e: e.activation(out=gcv, in_=gcv, func=AF.Silu), [b_gcv], [b_gcv])
            V(lambda e, ap_=ap_, ci=ci: e.tensor_tensor(out=act[ap_][:, ci, :], in0=gcv, in1=vcv, op=OP.mult),
              [b_gcv, b_vcv], [b_act[ap_][ci]])
            if ci == 3:
                io = next_slab()
                wd3 = slab[io][:, 0:4 * 2048].rearrange("p (k n) -> p k n", k=4, n=2048)
                wload(io, wd3, w_dn[grp * 512:(grp + 1) * 512, :].rearrange("(k p) n -> p k n", p=128))
                for cs in range(4):
                    for tt in range(NT):
                        k = 4 + (tt % 2)
                        pb = pbuf[k][0]
                        mm(bank(k), [(act[ap_][:, kk, tt * 128:(tt + 1) * 128], wd3[:, kk, cs * 512:(cs + 1) * 512])
                                     for kk in range(4)], [b_slab[io]] + b_act[ap_], [pb])
                        V(lambda e, k=k, tt=tt, cs=cs: e.tensor_tensor(out=hres[:, tt, cs * 512:(cs + 1) * 512], in0=bank(k),
                                                                       in1=hres[:, tt, cs * 512:(cs + 1) * 512], op=OP.add),
                          [pb], [b_h[tt][cs]])
        S.barrier()
        checkpoint(12, [(hres[:, tt, :], tt * 128) for tt in range(NT)])

        A.off = B
        fnw = A.alloc([D], F32)
        ot = [A.alloc([D], F32) for _ in range(2)]
        junk = A.alloc([D], BF16)
        ssf = A.alloc([NT], F32)
        assert A.off <= B + 40 * KB
        b_fnw, b_ssf, b_junk = Buf(), Buf(), Buf()
        b_ot = [Buf(), Buf()]
        b_out = [Buf() for _ in range(NT)]
        S.dma("sync", lambda e: e.dma_start(out=fnw, in_=fnw_d), writes=[b_fnw])
        for tt in range(NT):
            ACT(lambda e, tt=tt: e.activation(out=junk, in_=hres[:, tt, :], func=AF.Square, accum_out=ssf[:, tt:tt + 1]),
                [], [b_junk, b_ssf])
        ACT(lambda e: e.activation(out=ssf, in_=ssf, func=AF.Sqrt, scale=1.0 / D, bias=epsN), [b_ssf, b_const], [b_ssf])
        V(lambda e: e.reciprocal(out=ssf, in_=ssf), [b_ssf], [b_ssf])
        for tt in range(NT):
            q = tt % 2
            V(lambda e, tt=tt, q=q: e.scalar_tensor_tensor(out=ot[q], in0=hres[:, tt, :], scalar=ssf[:, tt:tt + 1], in1=fnw,
                                                           op0=OP.mult, op1=OP.mult), [b_ssf, b_fnw], [b_ot[q]])
            S.dma("sync", lambda e, tt=tt, q=q: e.dma_start(out=out_d[tt * 128:(tt + 1) * 128, :], in_=ot[q]),
                  reads=[b_ot[q]], writes=[b_out[tt]])
        S.barrier()
        S.emit()
      except _Stop:
        pass
    return nc


_NC_CACHE = {}


def _tables():
    gam = np.array([1.0 - 2.0 ** (-5.0 - h) for h in range(H)], np.float64)
    n = np.arange(128, dtype=np.float64)
    rel = n[None, :] - n[:, None]
    dt = np.zeros((128, H, 128), np.float64)
    for h in range(H):
        dt[:, h, :] = np.where(rel >= 0, gam[h] ** np.maximum(rel, 0), 0.0) * (128.0 ** -0.5)
    qdec = np.zeros((128, H, 128), np.float64)
    for h in range(H):
        qdec[:, h, :] = (gam[h] ** (n + 1.0))[None, :]
    kdec = np.zeros((128, H), np.float64)
    for h in range(H):
        kdec[:, h] = gam[h] ** (127.0 - n) * (128.0 ** -0.5)
    invf = (np.float32(10000.0) ** (-(np.arange(64, dtype=np.float32) * np.float32(2.0) / np.float32(128.0)))).astype(np.float32)
    return (dt.reshape(128, H * 128).astype(np.float32), qdec.reshape(128, H * 128).astype(np.float32),
            kdec.astype(np.float32), np.tile(invf[None, :], (128, 1)).astype(np.float32), gam)


def _fm(v):
    v = np.asarray(v, np.float32).reshape(-1)
    return np.ascontiguousarray(v.reshape(-1, 128).T)


def kernel(x, mem, positions, norm_mix_w, w_in, ret_gn_w, rnn_conv_w, rnn_conv_b,
           rg_w_a, rg_b_a, rg_w_x, rg_b_x, rg_lambda, w_proj_ret, w_proj_rnn, w_mix_out,
           norm_xa_w, norm_mem_w, xa_w_q, xa_w_k, xa_w_v, xa_w_o,
           norm_ffn_w, ffn_w_up, ffn_conv_w, ffn_conv_b, ffn_w_down, final_norm_w):
    f = lambda a: np.ascontiguousarray(np.asarray(a, np.float32))
    x = f(x)
    mem = f(mem)
    positions = np.asarray(positions, np.int32)
    dt, qdec, kdec, invf, gam = _tables()
    vecs = np.zeros((128, NV), np.float32)
    vecs[:, V_NMIX:V_NMIX + 16] = _fm(norm_mix_w[0])
    cw = f(rnn_conv_w[0])
    for j in range(4):
        vecs[:, V_CW + 16 * j:V_CW + 16 * j + 16] = _fm(cw[j])
    vecs[:, V_CB:V_CB + 16] = _fm(rnn_conv_b[0])
    vecs[:, V_BA:V_BA + 16] = _fm(rg_b_a[0])
    vecs[:, V_BX:V_BX + 16] = _fm(rg_b_x[0])
    vecs[:, V_LAM:V_LAM + 16] = _fm(rg_lambda[0])
    vecs[:, V_NXA:V_NXA + 16] = _fm(norm_xa_w[0])
    vecs[:, V_NMEM:V_NMEM + 16] = _fm(norm_mem_w[0])
    vecs[:, V_NFFN:V_NFFN + 16] = _fm(norm_ffn_w[0])
    vecs[:, V_GN:V_GN + 16] = _fm(ret_gn_w[0])
    fcw = f(ffn_conv_w[0])
    for j in range(3):
        vecs[:, V_FCW + 88 * j:V_FCW + 88 * j + 88] = _fm(fcw[j])
    vecs[:, V_FCB:V_FCB + 88] = _fm(ffn_conv_b[0])
    fnw = np.ascontiguousarray(np.tile(f(final_norm_w)[None, :], (128, 1)))
    ident = np.eye(128, dtype=np.float32)
    shared = {
        "vecs": vecs, "invf": invf, "dtab": dt, "qdec": qdec, "kdec": kdec, "ident": ident, "fnw": fnw,
        "w_in": f(w_in[0]), "rg_w_a": f(rg_w_a[0]), "rg_w_x": f(rg_w_x[0]),
        "w_proj_ret": f(w_proj_ret[0]), "w_proj_rnn": f(w_proj_rnn[0]), "w_mix_out": f(w_mix_out[0]),
        "xa_w_q": f(xa_w_q[0]), "xa_w_k": f(xa_w_k[0]), "xa_w_v": f(xa_w_v[0]), "xa_w_o": f(xa_w_o[0]),
        "ffn_w_up": f(ffn_w_up[0]), "ffn_w_down": f(ffn_w_down[0]),
    }
    if SMALLW:
        for kname in (() if STOP >= 20 else ("w_in",)) + ("rg_w_a", "rg_w_x", "w_proj_ret", "w_proj_rnn", "w_mix_out", "xa_w_q", "xa_w_k",
                      "xa_w_v", "xa_w_o", "ffn_w_up", "ffn_w_down"):
            shared[kname] = np.zeros((1, 1), np.float32)
    in_maps = []
    for c in range(8):
        b, j = divmod(c, 4)
        s0 = j * T
        xh = x[b, s0 - 128:s0] if j > 0 else np.zeros((128, D), np.float32)
        coefr = np.zeros((4, H), np.float64)
        for r in range(4):
            if r < j:
                coefr[r] = gam ** (1024.0 * (j - 1 - r))
        maskl = np.array([1.0 if r < j else 0.0 for r in range(4)], np.float32)
        maskp = np.array([1.0 if r == j - 1 else 0.0 for r in range(4)], np.float32)
        m = dict(shared)
        m.update({
            "x": np.ascontiguousarray(x[b, s0:s0 + T]),
            "xh": np.ascontiguousarray(xh),
            "mem": np.ascontiguousarray(mem[b]),
            "pos": np.ascontiguousarray(positions[b, s0:s0 + T].reshape(NT, 128).T),
            "coefr": np.ascontiguousarray(np.tile(coefr.reshape(1, 32).astype(np.float32), (128, 1))),
            "maskl": np.ascontiguousarray(np.tile(maskl[None, :], (128, 1))),
            "maskp": np.ascontiguousarray(np.tile(maskp[None, :], (128, 1))),
        })
        in_maps.append(m)
    if "nc" not in _NC_CACHE:
        _NC_CACHE["nc"] = build_nc()
    res = run_bass_kernel_spmd(_NC_CACHE["nc"], in_maps, core_ids=list(range(8)))
    out = np.zeros((2, 4 * T, D), np.float32)
    for c in range(8):
        b, j = divmod(c, 4)
        out[b, j * T:(j + 1) * T] = res.results[c]["out"]
    return out
```

```python
import contextlib
import os
import numpy as np
import concourse.bass as bass
import concourse.mybir as mybir
from concourse.bass_utils import run_bass_kernel_spmd

F32 = mybir.dt.float32
BF16 = mybir.dt.bfloat16
I32 = mybir.dt.int32
U8 = mybir.dt.uint8
AF = mybir.ActivationFunctionType
OP = mybir.AluOpType

T = 1024
NT = 8
D = 2048
KC = 16
DFF = 5632
NFC = 44
STOP = int(os.environ.get("KSTOP", "-1"))


SMALLW = STOP in (-3, -2, 0, 20, 21, 22, 23)


class _Stop(Exception):
    pass
H = 8
TWO_PI = 6.283185307179586
C1 = 6.28125
C2 = TWO_PI - C1

V_NMIX, V_CW, V_CB, V_BA, V_BX, V_LAM, V_NXA, V_NMEM, V_NFFN, V_GN, V_FCW, V_FCB = (
    0, 16, 80, 96, 112, 128, 144, 160, 176, 192, 208, 472)
NV = 560


class Buf:
    __slots__ = ("name", "w", "r", "excl")

    def __init__(self, name="", excl=False):
        self.name = name
        self.w = None
        self.r = []
        self.excl = excl


class Sched:
    ENGS = ("tensor", "vector", "scalar", "gpsimd", "sync")

    def __init__(self, nc, n_dma_sems=24, same_engine_sync=True):
        self.nc = nc
        self.same_engine_sync = same_engine_sync
        self.prog = {e: [] for e in self.ENGS}
        self.cnt = {e: 0 for e in self.ENGS}
        self.seen = {e: {} for e in self.ENGS}
        self.n_dma_sems = n_dma_sems
        self.dma_val = [0] * n_dma_sems
        self.dma_rr = 0
        self.sw_rr = 0

    def _need(self, eng, toks):
        seen = self.seen[eng]
        best = {}
        for t in toks:
            if t is None:
                continue
            k, v = t
            if k == eng and (eng == "tensor" or not self.same_engine_sync):
                continue
            if seen.get(k, 0) >= v:
                continue
            if best.get(k, 0) < v:
                best[k] = v
        for k, v in best.items():
            seen[k] = v
        return list(best.items())

    @staticmethod
    def _deps(reads, writes, eng=None):
        toks = []
        for b in reads:
            toks.append(b.w)
            if b.excl:
                toks.extend(t for t in b.r if t[0] != eng)
        for b in writes:
            toks.append(b.w)
            toks.extend(b.r)
        return toks

    @staticmethod
    def _commit(tok, reads, writes):
        for b in reads:
            b.r.append(tok)
        for b in writes:
            b.w = tok
            b.r = []

    def op(self, eng, fn, reads=(), writes=()):
        waits = self._need(eng, self._deps(reads, writes, eng))
        self.cnt[eng] += 1
        tok = (eng, self.cnt[eng])
        self.prog[eng].append((waits, fn, ("eng", eng)))
        self._commit(tok, reads, writes)
        return tok

    def dma(self, q, fn, reads=(), writes=()):
        half = self.n_dma_sems // 2
        if q == "gpsimd":
            s = self.sw_rr
            self.sw_rr = (self.sw_rr + 1) % half
        else:
            s = half + self.dma_rr
            self.dma_rr = (self.dma_rr + 1) % (self.n_dma_sems - half)
        key = ("dma", s)
        toks = self._deps(reads, writes, q)
        if self.dma_val[s] > 0:
            toks.append((key, self.dma_val[s]))
        waits = self._need(q, toks)
        self.dma_val[s] += 16
        tok = (key, self.dma_val[s])
        self.prog[q].append((waits, fn, ("dma", s)))
        self._commit(tok, reads, writes)
        return tok

    def barrier(self):
        toks = [(e, self.cnt[e]) for e in self.ENGS if self.cnt[e] > 0]
        toks += [(("dma", s), v) for s, v in enumerate(self.dma_val) if v > 0]
        for e in self.ENGS:
            waits = self._need(e, toks)
            if waits:
                self.prog[e].append((waits, None, None))

    def emit(self):
        nc = self.nc
        with contextlib.ExitStack() as st:
            esem = {e: st.enter_context(nc.semaphore("s_" + e)) for e in self.ENGS}
            dsem = [st.enter_context(nc.semaphore("d_%d" % i)) for i in range(self.n_dma_sems)]

            def semof(k):
                return esem[k] if isinstance(k, str) else dsem[k[1]]

            block = st.enter_context(nc.Block())

            def body(engname):
                def run(e):
                    for waits, fn, inc in self.prog[engname]:
                        for k, v in waits:
                            e.wait_ge(semof(k), v)
                        if fn is None:
                            continue
                        ins = fn(e)
                        if inc[0] == "eng":
                            ins.then_inc(esem[inc[1]], 1)
                        else:
                            ins.then_inc(dsem[inc[1]], 16)
                return run

            block.tensor(body("tensor"))
            block.vector(body("vector"))
            block.scalar(body("scalar"))
            block.gpsimd(body("gpsimd"))
            block.sync(body("sync"))


_DS = {F32: 4, BF16: 2, I32: 4, U8: 1}


class Arena:
    def __init__(self, t, size):
        self.t = t
        self.size = size
        self.off = 0

    def alloc(self, shape, dtype):
        n = int(np.prod(shape))
        nb = n * _DS[dtype]
        off = (self.off + 63) // 64 * 64
        assert off + nb <= self.size, ("arena overflow", off, nb, self.size)
        v = self.t[:, off:off + nb]
        if dtype != U8:
            v = v.bitcast(dtype)
        if len(shape) == 2:
            v = v.rearrange("p (a b) -> p a b", a=shape[0], b=shape[1])
        elif len(shape) == 3:
            v = v.rearrange("p (a b c) -> p a b c", a=shape[0], b=shape[1], c=shape[2])
        self.off = off + nb
        return v


def build_nc():
    nc = bass.Bass("TRN2", target_bir_lowering=False)

    def din(name, shape, dt=F32):
        return nc.dram_tensor(name, list(shape), dt, kind="ExternalInput").ap()

    x_d = din("x", [T, D])
    xh_d = din("xh", [128, D])
    mem_d = din("mem", [256, D])
    pos_d = din("pos", [128, NT], I32)
    vecs_d = din("vecs", [128, NV])
    invf_d = din("invf", [128, 64])
    dt_d = din("dtab", [128, H * 128])
    qdec_d = din("qdec", [128, H * 128])
    kdec_d = din("kdec", [128, H])
    coef_d = din("coefr", [128, 32])
    mskl_d = din("maskl", [128, 4])
    mskp_d = din("maskp", [128, 4])
    ident_d = din("ident", [128, 128])
    fnw_d = din("fnw", [128, D])
    _din = din

    def din(name, shape, dt=F32):
        return _din(name, [1, 1] if SMALLW else shape, dt)
    w_in = _din("w_in", [D, 14336]) if STOP >= 20 else din("w_in", [D, 14336])
    rg_wa = din("rg_w_a", [16, 128, 128])
    rg_wx = din("rg_w_x", [16, 128, 128])
    w_pret = din("w_proj_ret", [D, D])
    w_prnn = din("w_proj_rnn", [D, D])
    w_mix = din("w_mix_out", [D, D])
    w_xq = din("xa_w_q", [D, D])
    w_xk = din("xa_w_k", [D, D])
    w_xv = din("xa_w_v", [D, D])
    w_xo = din("xa_w_o", [D, D])
    w_up = din("ffn_w_up", [D, 2 * DFF])
    w_dn = din("ffn_w_down", [DFF, D])
    out_d = nc.dram_tensor("out", [T, D], F32, kind="ExternalOutput").ap()

    gin1 = nc.dram_tensor("gin1", [H * 128, 256], F32)
    gout1 = nc.dram_tensor("gout1", [4 * H * 128, 256], F32)
    gin2 = nc.dram_tensor("gin2", [128, 32], F32)
    gout2 = nc.dram_tensor("gout2", [4 * 128, 32], F32)
    gin3 = nc.dram_tensor("gin3", [128, 32], F32)
    gout3 = nc.dram_tensor("gout3", [4 * 128, 32], F32)
    RG = [[0, 1, 2, 3], [4, 5, 6, 7]]

    gam = [1.0 - 2.0 ** (-5.0 - h) for h in range(H)]

    ARENA = 207 * 1024
    st = contextlib.ExitStack()
    with st:
      try:
        arena_t = st.enter_context(nc.sbuf_tensor("arena", [128, ARENA], U8))
        psum_t = st.enter_context(nc.psum_tensor("ps", [128, 4096], F32))
        S = Sched(nc)
        A = Arena(arena_t, ARENA)
        ps = psum_t[:, :]
        psb = psum_t[:, :].bitcast(BF16)

        def bank(i, n=512, off=0):
            return ps[:, i * 512 + off: i * 512 + off + n]

        pbuf = [[Buf("ps%d" % i, excl=True)] * 4 for i in range(8)]

        ident = A.alloc([128], BF16)
        ones = A.alloc([128], BF16)
        vecs = A.alloc([NV], F32)
        nsp8 = A.alloc([16], F32)
        nsp16 = A.alloc([16], F32)
        hn = A.alloc([KC, 1152], BF16)
        NSLAB = 2
        slab = [A.alloc([KC * 512], BF16) for _ in range(NSLAB)]
        b_slab = [Buf("slab%d" % i) for i in range(NSLAB)]
        b_hn = [Buf("hn%d" % i) for i in range(9)]
        b_const = Buf("const")
        persist_end = A.off
        slab_rr = [0]

        def next_slab():
            i = slab_rr[0]
            slab_rr[0] = (i + 1) % NSLAB
            return i

        def slab3(i, ncols):
            return slab[i][:, 0:KC * ncols].rearrange("p (k n) -> p k n", k=KC, n=ncols)

        def wload(i, dst, src, extra_reads=()):
            S.dma("gpsimd", lambda e: e.dma_start(out=dst, in_=src), reads=list(extra_reads), writes=[b_slab[i]])

        def wcols(W, c0, n):
            return W[:, c0:c0 + n].rearrange("(k p) n -> p k n", p=128)

        def mm(out, pairs, reads, writes):
            def fn(e):
                n = len(pairs)
                ins = None
                for i, (l, r) in enumerate(pairs):
                    ins = e.matmul(out, l, r, start=(i == 0), stop=(i == n - 1))
                return ins
            return S.op("tensor", fn, reads, writes)

        def V(fn, reads, writes):
            return S.op("vector", fn, reads, writes)

        def ACT(fn, reads, writes):
            return S.op("scalar", fn, reads, writes)

        def POOL(fn, reads, writes):
            return S.op("gpsimd", fn, reads, writes)

        def vcol(base, kc):
            return vecs[:, base + kc: base + kc + 1]


        def checkpoint(k, dumps=()):
            if STOP != k:
                return
            S.barrier()
            for ap2, row0 in dumps:
                n = ap2.shape[1]
                S.dma("gpsimd", lambda e, ap2=ap2, row0=row0, n=n: e.dma_start(out=out_d[row0:row0 + 128, 0:n], in_=ap2),
                      writes=[Buf()])
            S.barrier()
            S.emit()
            raise _Stop()

        S.dma("sync", lambda e: e.dma_start(out=vecs, in_=vecs_d), writes=[b_const])
        S.dma("gpsimd", lambda e: e.dma_start(out=ident, in_=ident_d), writes=[b_const])
        V(lambda e: e.memset(ones, 1.0), [], [b_const])

        checkpoint(-2, [(vecs[:, 0:NV], 0)])
        def rmsnorm_to_fm(tiles, wbase, dst_cols, tmpA):
            n = len(tiles)
            ss, rstd = tmpA["ss"], tmpA["rstd"]
            b_ss = Buf("ss")
            for i, (src, sb, _, _) in enumerate(tiles):
                jb = tmpA["b_junk"][i % 2]
                ACT(lambda e, src=src, i=i: e.activation(out=tmpA["junk"][i % 2], in_=src, func=AF.Square,
                                                         accum_out=ss[:, i:i + 1]),
                    [sb], [jb, b_ss])
            ACT(lambda e: e.activation(out=rstd[:, 0:n], in_=ss[:, 0:n], func=AF.Sqrt, scale=1.0 / D, bias=tmpA["eps"]),
                [b_ss, b_const], [b_ss])
            V(lambda e: e.reciprocal(out=rstd[:, 0:n], in_=rstd[:, 0:n]), [b_ss], [b_ss])
            for i, (src, sb, col0, hb) in enumerate(tiles):
                xs = tmpA["xs"][i % 2]
                xb = tmpA["b_xs"][i % 2]
                ACT(lambda e, src=src, i=i, xs=xs: e.activation(out=xs, in_=src, func=AF.Copy, scale=rstd[:, i:i + 1]),
                    [sb, b_ss], [xb])
                for g4 in range(4):
                    pb = pbuf[2 + g4][0]
                    pv = psb[:, (2 + g4) * 1024:(2 + g4) * 1024 + 512]

                    def tfn(e, xs=xs, g4=g4, pv=pv):
                        ins = None
                        for q in range(4):
                            kc = g4 * 4 + q
                            ins = e.transpose(pv[:, q * 128:(q + 1) * 128], xs[:, kc * 128:(kc + 1) * 128], ident)
                        return ins
                    S.op("tensor", tfn, [xb, b_const], [pb])
                    wv = vecs[:, wbase + g4 * 4: wbase + g4 * 4 + 4].unsqueeze(2).to_broadcast([128, 4, 128])
                    V(lambda e, pv=pv, g4=g4, col0=col0, wv=wv: e.tensor_tensor(
                        out=hn[:, g4 * 4:g4 * 4 + 4, col0:col0 + 128],
                        in0=pv.rearrange("p (a b) -> p a b", a=4, b=128), in1=wv, op=OP.mult),
                      [pb, b_const], [hb])

        epsN = A.alloc([1], F32)
        epsG = A.alloc([1], F32)
        persist_end = A.off
        V(lambda e: e.memset(epsN, 1e-6), [], [b_const])
        V(lambda e: e.memset(epsG, 1e-5), [], [b_const])

        def norm_tmp():
            return {"ss": A.alloc([16], F32), "rstd": A.alloc([16], F32), "eps": epsN,
                    "junk": [A.alloc([D], BF16) for _ in range(2)], "b_junk": [Buf(), Buf()],
                    "xs": [A.alloc([D], BF16) for _ in range(2)], "b_xs": [Buf(), Buf()]}

        cos2 = A.alloc([NT, 128], F32)
        sinm = A.alloc([NT, 128], F32)
        dtab = A.alloc([H, 128], F32)
        qdec = A.alloc([H, 128], F32)
        kdec = A.alloc([H], F32)
        m1_base = A.off
        b_tab = Buf("tab")
        S.dma("sync", lambda e: e.dma_start(out=dtab.rearrange("p a b -> p (a b)"), in_=dt_d), writes=[b_tab])
        S.dma("sync", lambda e: e.dma_start(out=qdec.rearrange("p a b -> p (a b)"), in_=qdec_d), writes=[b_tab])
        S.dma("sync", lambda e: e.dma_start(out=kdec, in_=kdec_d), writes=[b_tab])

        nt = norm_tmp()
        xt = [A.alloc([D], F32) for _ in range(2)]
        b_xt = [Buf(), Buf()]
        posi = A.alloc([NT], I32)
        posf = A.alloc([NT], F32)
        invf = A.alloc([64], F32)
        ang = A.alloc([NT, 2, 64], F32)
        kf = A.alloc([NT, 2, 64], F32)
        ki = A.alloc([NT, 2, 64], I32)
        msk = A.alloc([NT, 2, 64], F32)
        S.dma("sync", lambda e: e.dma_start(out=posi, in_=pos_d), writes=[b_tab])
        S.dma("sync", lambda e: e.dma_start(out=invf, in_=invf_d), writes=[b_tab])
        V(lambda e: e.tensor_copy(out=posf, in_=posi), [b_tab], [b_tab])
        for tt in range(NT):
            V(lambda e, tt=tt: e.tensor_scalar(out=ang[:, tt, 0, :], in0=invf, scalar1=posf[:, tt:tt + 1], scalar2=None,
                                               op0=OP.mult), [b_tab], [b_tab])
        a0 = ang[:, :, 0, :]
        a1 = ang[:, :, 1, :]
        k0 = kf[:, :, 0, :]
        ki0 = ki[:, :, 0, :]
        m0 = msk[:, :, 0, :]
        V(lambda e: e.tensor_scalar(out=k0, in0=a0, scalar1=1.0 / TWO_PI, scalar2=None, op0=OP.mult), [b_tab], [b_tab])
        V(lambda e: e.tensor_copy(out=ki0, in_=k0), [b_tab], [b_tab])
        V(lambda e: e.tensor_copy(out=k0, in_=ki0), [b_tab], [b_tab])
        V(lambda e: e.scalar_tensor_tensor(out=a0, in0=k0, scalar=-C1, in1=a0, op0=OP.mult, op1=OP.add), [b_tab], [b_tab])
        V(lambda e: e.scalar_tensor_tensor(out=a0, in0=k0, scalar=-C2, in1=a0, op0=OP.mult, op1=OP.add), [b_tab], [b_tab])
        V(lambda e: e.tensor_scalar(out=m0, in0=a0, scalar1=np.pi, scalar2=None, op0=OP.is_gt), [b_tab], [b_tab])
        V(lambda e: e.scalar_tensor_tensor(out=a0, in0=m0, scalar=-TWO_PI, in1=a0, op0=OP.mult, op1=OP.add), [b_tab], [b_tab])
        V(lambda e: e.tensor_scalar(out=m0, in0=a0, scalar1=-np.pi, scalar2=None, op0=OP.is_lt), [b_tab], [b_tab])
        V(lambda e: e.scalar_tensor_tensor(out=a0, in0=m0, scalar=TWO_PI, in1=a0, op0=OP.mult, op1=OP.add), [b_tab], [b_tab])
        m1 = msk[:, :, 1, :]
        V(lambda e: e.tensor_scalar(out=a1, in0=a0, scalar1=np.pi / 2, scalar2=None, op0=OP.add), [b_tab], [b_tab])
        V(lambda e: e.tensor_scalar(out=m1, in0=a1, scalar1=np.pi, scalar2=None, op0=OP.is_gt), [b_tab], [b_tab])
        V(lambda e: e.scalar_tensor_tensor(out=a1, in0=m1, scalar=-TWO_PI, in1=a1, op0=OP.mult, op1=OP.add), [b_tab], [b_tab])
        V(lambda e: e.tensor_scalar(out=ang, in0=ang, scalar1=3.14159, scalar2=-3.14159, op0=OP.min, op1=OP.max),
          [b_tab], [b_tab])
        ACT(lambda e: e.activation(out=ang, in_=ang, func=AF.Sin), [b_tab], [b_tab])
        V(lambda e: e.tensor_copy(out=cos2[:, :, 0:64], in_=a1), [b_tab], [b_tab])
        V(lambda e: e.tensor_copy(out=cos2[:, :, 64:128], in_=a1), [b_tab], [b_tab])
        V(lambda e: e.tensor_copy(out=sinm[:, :, 64:128], in_=a0), [b_tab], [b_tab])
        V(lambda e: e.tensor_scalar(out=sinm[:, :, 0:64], in0=a0, scalar1=-1.0, scalar2=None, op0=OP.mult), [b_tab], [b_tab])
        ACT(lambda e: e.activation(out=nsp8, in_=vecs[:, V_LAM:V_LAM + 16], func=AF.Exp, scale=-1.0), [b_const], [b_const])
        ACT(lambda e: e.activation(out=nsp8, in_=nsp8, func=AF.Ln, bias=1.0), [b_const], [b_const])
        V(lambda e: e.tensor_scalar(out=nsp16, in0=nsp8, scalar1=-16.0, scalar2=None, op0=OP.mult), [b_const], [b_const])
        V(lambda e: e.tensor_scalar(out=nsp8, in0=nsp8, scalar1=-8.0, scalar2=None, op0=OP.mult), [b_const], [b_const])

        checkpoint(-3, [(cos2.rearrange('p a b -> p (a b)'), 256), (sinm.rearrange('p a b -> p (a b)'), 384), (nsp8, 0)])
        tiles = []
        for i in range(9):
            src_d = xh_d if i == 0 else x_d[(i - 1) * 128:i * 128, :]
            S.dma("sync", lambda e, i=i, src_d=src_d: e.dma_start(out=xt[i % 2], in_=src_d), writes=[b_xt[i % 2]])
            rmsnorm_to_fm([(xt[i % 2], b_xt[i % 2], i * 128, b_hn[i])], V_NMIX, None,
                          {**nt, "ss": nt["ss"][:, i:i + 1], "rstd": nt["rstd"][:, i:i + 1]})
        S.barrier()
        checkpoint(0, [(hn[:, 0, 0:1152], 0), (hn[:, 15, 0:1152], 128), (cos2.rearrange('p a b -> p (a b)'), 256), (sinm.rearrange('p a b -> p (a b)'), 384)])

        A.off = m1_base
        oloc = A.alloc([NT, D], BF16)
        qts = A.alloc([H, T], BF16)
        qkr = [A.alloc([NT, 256], BF16) for _ in range(2)]
        kd = [A.alloc([NT, 128], BF16) for _ in range(2)]
        vv = [A.alloc([NT, 256], BF16) for _ in range(2)]
        qkT = [A.alloc([NT, 256], BF16) for _ in range(2)]
        tA = [A.alloc([256], F32) for _ in range(2)]
        tB = [A.alloc([256], F32) for _ in range(2)]
        sT = [A.alloc([128], BF16) for _ in range(2)]
        Sst = [A.alloc([256], F32) for _ in range(2)]
        Sbf = [A.alloc([256], BF16) for _ in range(2)]
        b_oloc = [[Buf() for _ in range(H)] for _ in range(NT)]
        b_qts = [[Buf() for _ in range(NT)] for _ in range(H)]
        b_qkr = [[Buf() for _ in range(NT)] for _ in range(2)]
        b_kd = [[Buf() for _ in range(NT)] for _ in range(2)]
        b_vv = [[Buf() for _ in range(NT)] for _ in range(2)]
        b_qkT = [[Buf() for _ in range(NT)] for _ in range(2)]
        b_tA = [Buf(), Buf()]
        b_tB = [Buf(), Buf()]
        b_sT = [Buf(), Buf()]
        b_S = [Buf(), Buf()]
        b_Sbf = [Buf(), Buf()]
        b_gin1 = [Buf() for _ in range(H)]
        slab_of_head = {}

        def m1_load(h):
            i = next_slab()
            slab_of_head[h] = i
            s3 = slab3(i, 512)
            wload(i, s3[:, :, 0:128], wcols(w_in, h * 128, 128))
            wload(i, s3[:, :, 128:256], wcols(w_in, 1024 + h * 128, 128))
            wload(i, s3[:, :, 256:512], wcols(w_in, 2048 + h * 256, 256))

        def m1_proj(h, tt):
            i = slab_of_head[h]
            s3 = slab3(i, 512)
            p = h % 2
            bk = tt % 2
            pb = pbuf[bk][0]
            c0 = 128 + tt * 128
            mm(bank(bk), [(hn[:, kc, c0:c0 + 128], s3[:, kc, :]) for kc in range(KC)],
               [b_hn[tt + 1], b_slab[i]], [pb])
            qk = bank(bk, 256).rearrange("p (a b) -> p a b", a=2, b=128)
            ta = tA[tt % 2].rearrange("p (a b) -> p a b", a=2, b=128)
            tb = tB[tt % 2].rearrange("p (a b) -> p a b", a=2, b=128)
            cb = cos2[:, tt, :].unsqueeze(1).to_broadcast([128, 2, 128])
            V(lambda e: e.tensor_tensor(out=ta, in0=qk, in1=cb, op=OP.mult), [pb, b_tab], [b_tA[tt % 2]])
            V(lambda e: e.tensor_tensor(out=tb[:, :, 0:64], in0=qk[:, :, 64:128],
                                        in1=sinm[:, tt, 0:64].unsqueeze(1).to_broadcast([128, 2, 64]), op=OP.mult),
              [pb, b_tab], [b_tB[tt % 2]])
            V(lambda e: e.tensor_tensor(out=tb[:, :, 64:128], in0=qk[:, :, 0:64],
                                        in1=sinm[:, tt, 64:128].unsqueeze(1).to_broadcast([128, 2, 64]), op=OP.mult),
              [pb, b_tab], [b_tB[tt % 2]])
            ACT(lambda e: e.copy(out=vv[p][:, tt, :], in_=bank(bk, 256, 256)), [pb], [b_vv[p][tt]])
            V(lambda e: e.tensor_tensor(out=qkr[p][:, tt, :], in0=tA[tt % 2], in1=tB[tt % 2], op=OP.add),
              [b_tA[tt % 2], b_tB[tt % 2]], [b_qkr[p][tt]])
            V(lambda e: e.tensor_scalar(out=kd[p][:, tt, :], in0=qkr[p][:, tt, 128:256], scalar1=kdec[:, h:h + 1],
                                        scalar2=None, op0=OP.mult), [b_qkr[p][tt], b_tab], [b_kd[p][tt]])
            sl = tt % 2
            pv = psb[:, (2 + sl) * 1024:(2 + sl) * 1024 + 256]
            ptb = pbuf[2 + sl][0]

            def tfn(e):
                e.transpose(pv[:, 0:128], qkr[p][:, tt, 0:128], ident)
                return e.transpose(pv[:, 128:256], qkr[p][:, tt, 128:256], ident)
            S.op("tensor", tfn, [b_qkr[p][tt], b_const], [ptb])
            ACT(lambda e: e.copy(out=qkT[p][:, tt, :], in_=pv), [ptb], [b_qkT[p][tt]])
            V(lambda e: e.tensor_tensor(out=qts[:, h, tt * 128:(tt + 1) * 128], in0=pv[:, 0:128], in1=qdec[:, h, :],
                                        op=OP.mult), [ptb, b_tab], [b_qts[h][tt]])

        def m1_chunk(h, c):
            p = h % 2
            sl = c % 2
            g128 = gam[h] ** 128
            psc = pbuf[(4, 7)[sl]][0]
            sc = bank((4, 7)[sl], 128)
            mm(sc, [(qkT[p][:, c, 128:256], qkT[p][:, c, 0:128])], [b_qkT[p][c]], [psc])
            V(lambda e: e.tensor_tensor(out=sT[sl], in0=sc, in1=dtab[:, h, :], op=OP.mult), [psc, b_tab], [b_sT[sl]])
            po = pbuf[5][0]
            ob = bank(5, 256)
            pairs = [(sT[sl], vv[p][:, c, :])]
            rds = [b_sT[sl], b_vv[p][c]]
            if c > 0:
                pairs.append((qts[:, h, c * 128:(c + 1) * 128], Sbf[p]))
                rds += [b_qts[h][c], b_Sbf[p]]
            mm(ob, pairs, rds, [po])
            ACT(lambda e: e.copy(out=oloc[:, c, h * 256:(h + 1) * 256], in_=ob), [po], [b_oloc[c][h]])
            pk = pbuf[6][0]
            kb = bank(6, 256)
            mm(kb, [(kd[p][:, c, :], vv[p][:, c, :])], [b_kd[p][c], b_vv[p][c]], [pk])
            if c == 0:
                V(lambda e: e.tensor_copy(out=Sst[p], in_=kb), [pk], [b_S[p]])
            else:
                V(lambda e: e.scalar_tensor_tensor(out=Sst[p], in0=Sst[p], scalar=g128, in1=kb, op0=OP.mult, op1=OP.add),
                  [pk], [b_S[p]])
            if c < NT - 1:
                ACT(lambda e: e.copy(out=Sbf[p], in_=Sst[p]), [b_S[p]], [b_Sbf[p]])
            else:
                S.dma("sync", lambda e: e.dma_start(out=gin1.ap()[h * 128:(h + 1) * 128, :], in_=Sst[p]),
                      reads=[b_S[p]], writes=[b_gin1[h]])

        m1_load(0)
        for h in range(H + 1):
            if h + 1 < H:
                m1_load(h + 1)
            for step in range(NT):
                if h < H:
                    m1_proj(h, step)
                if h >= 1:
                    m1_chunk(h - 1, step)
                if h == 1 and step == 0:
                    checkpoint(21, [(oloc[:, 0, :], 0), (qkT[0].rearrange('p a b -> p (a b)'), 128),
                                    (Sst[0], 256), (sT[0], 384)])
            if h == 0:
                checkpoint(20, [(qkr[0].rearrange('p a b -> p (a b)'), 0), (vv[0].rearrange('p a b -> p (a b)'), 128),
                                (kd[0].rearrange('p a b -> p (a b)'), 256), (qkT[0].rearrange('p a b -> p (a b)'), 384),
                                (qts[:, 0, :], 512)])
            if h == 1:
                checkpoint(22, [(oloc[:, tt, :], tt * 128) for tt in range(NT)])

        checkpoint(1, [(oloc[:, tt, :], tt * 128) for tt in range(NT)])
        S.barrier()
        B = m1_base
        KB = 1024
        hs_d = nc.dram_tensor("hs_d", [KC, 128, T], F32)
        ac_d = nc.dram_tensor("ac_d", [KC, 128, T], F32)
        A.off = B + 52 * KB
        wga = A.alloc([16, 128], BF16)
        wgx = A.alloc([16, 128], BF16)
        xc = [A.alloc([T], F32) for _ in range(2)]
        xcb = A.alloc([T], BF16)
        ra = [A.alloc([T], F32) for _ in range(2)]
        a2 = A.alloc([T], F32)
        iu = [A.alloc([T], F32) for _ in range(2)]
        hsc = [A.alloc([T], F32) for _ in range(2)]
        asc = [A.alloc([T], F32) for _ in range(2)]
        zer = A.alloc([T], BF16)
        ext = [A.alloc([1032], F32) for _ in range(2)]
        ends = A.alloc([32], F32)
        b_wg = Buf()
        b_xcb, b_a2, b_zer, b_ends = (Buf() for _ in range(4))
        b_xc = [Buf(), Buf()]
        b_ra = [Buf(), Buf()]
        b_iu = [Buf(), Buf()]
        b_ext = [Buf(), Buf()]
        b_hsc = [Buf(), Buf()]
        b_asc = [Buf(), Buf()]
        b_hsd = [Buf() for _ in range(KC)]
        b_acd = [Buf() for _ in range(KC)]
        S.dma("gpsimd", lambda e: e.dma_start(out=wga, in_=rg_wa.rearrange("g i j -> i g j")), writes=[b_wg])
        S.dma("gpsimd", lambda e: e.dma_start(out=wgx, in_=rg_wx.rearrange("g i j -> i g j")), writes=[b_wg])
        V(lambda e: e.memset(zer, 0.0), [], [b_zer])
        hn_all = [b_hn[i] for i in range(9)]
        m4_slabs = {}
        for g0 in (0, 4):
            m4_slabs[g0] = next_slab()
            wload(m4_slabs[g0], slab3(m4_slabs[g0], 512), wcols(w_in, 6144 + g0 * 128, 512))
        b_gout1 = Buf("gout1")
        POOL(lambda e: e.collective_compute("AllGather", OP.bypass, replica_groups=RG,
                                            ins=[gin1.ap()], outs=[gout1.ap()]), b_gin1, [b_gout1])
        cur = None
        for g in range(KC):
            if g % 4 == 0:
                if g in m4_slabs:
                    cur = m4_slabs[g]
                else:
                    cur = next_slab()
                    wload(cur, slab3(cur, 512), wcols(w_in, 6144 + g * 128, 512))
            s3 = slab3(cur, 512)
            wsl = slice((g % 4) * 128, (g % 4 + 1) * 128)
            q = g % 2
            pX = [pbuf[0][0], pbuf[1][0], pbuf[2][0]]
            mm(bank(0), [(s3[:, kc, wsl], hn[:, kc, 125:637]) for kc in range(KC)], [b_slab[cur]] + hn_all, [pX[0]])
            mm(bank(1), [(s3[:, kc, wsl], hn[:, kc, 637:1149]) for kc in range(KC)], [b_slab[cur]] + hn_all, [pX[1]])
            mm(bank(2, 3), [(s3[:, kc, wsl], hn[:, kc, 1149:1152]) for kc in range(KC)], [b_slab[cur]] + hn_all, [pX[2]])
            ACT(lambda e, q=q: e.copy(out=ext[q][:, 0:1024], in_=ps[:, 0:1024]), [pX[0], pX[1]], [b_ext[q]])
            ACT(lambda e, q=q: e.copy(out=ext[q][:, 1024:1027], in_=bank(2, 3)), [pX[2]], [b_ext[q]])
            V(lambda e, g=g, q=q: e.tensor_scalar(out=xc[q], in0=ext[q][:, 0:1024], scalar1=vcol(V_CW, g), scalar2=vcol(V_CB, g),
                                                  op0=OP.mult, op1=OP.add), [b_ext[q], b_const], [b_xc[q]])
            for j in range(1, 4):
                V(lambda e, g=g, j=j, q=q: e.scalar_tensor_tensor(out=xc[q], in0=ext[q][:, j:j + 1024],
                                                                  scalar=vcol(V_CW + 16 * j, g), in1=xc[q],
                                                                  op0=OP.mult, op1=OP.add), [b_ext[q], b_const], [b_xc[q]])
            ACT(lambda e, q=q: e.copy(out=xcb, in_=xc[q]), [b_xc[q]], [b_xcb])
            pG = [pbuf[3][0], pbuf[4][0], pbuf[5][0], pbuf[6][0]]
            for th in range(2):
                mm(bank(3 + th), [(wga[:, g, :], xcb[:, th * 512:(th + 1) * 512])], [b_wg, b_xcb], [pG[th]])
                mm(bank(5 + th), [(wgx[:, g, :], xcb[:, th * 512:(th + 1) * 512])], [b_wg, b_xcb], [pG[2 + th]])
            ACT(lambda e, g=g, q=q: e.activation(out=ra[q], in_=ps[:, 3 * 512:5 * 512], func=AF.Sigmoid, bias=vcol(V_BA, g)),
                [pG[0], pG[1], b_const], [b_ra[q]])
            ACT(lambda e, g=g, q=q: e.activation(out=iu[q], in_=ps[:, 5 * 512:7 * 512], func=AF.Sigmoid, bias=vcol(V_BX, g)),
                [pG[2], pG[3], b_const], [b_iu[q]])
            ACT(lambda e, g=g, q=q: e.activation(out=a2, in_=ra[q], func=AF.Exp, scale=nsp16[:, g:g + 1]),
                [b_ra[q], b_const], [b_a2])
            ACT(lambda e, g=g, q=q: e.activation(out=ra[q], in_=ra[q], func=AF.Exp, scale=nsp8[:, g:g + 1]),
                [b_ra[q], b_const], [b_ra[q]])
            ACT(lambda e: e.activation(out=a2, in_=a2, func=AF.Sqrt, scale=-1.0, bias=1.0), [b_a2], [b_a2])
            V(lambda e, q=q: e.tensor_tensor(out=iu[q], in0=iu[q], in1=xc[q], op=OP.mult), [b_iu[q], b_xc[q]], [b_iu[q]])
            V(lambda e, q=q: e.tensor_tensor(out=iu[q], in0=iu[q], in1=a2, op=OP.mult), [b_iu[q], b_a2], [b_iu[q]])
            V(lambda e, q=q: e.tensor_tensor_scan(out=hsc[q], data0=ra[q], data1=iu[q], initial=0.0, op0=OP.mult, op1=OP.add),
              [b_ra[q], b_iu[q]], [b_hsc[q]])
            V(lambda e, g=g, q=q: e.tensor_copy(out=ends[:, 16 + g:17 + g], in_=hsc[q][:, T - 1:T]), [b_hsc[q]], [b_ends])
            S.dma("sync", lambda e, g=g, q=q: e.dma_start(out=hs_d.ap()[g], in_=hsc[q]), reads=[b_hsc[q]], writes=[b_hsd[g]])
            V(lambda e, q=q: e.tensor_tensor_scan(out=asc[q], data0=ra[q], data1=zer, initial=1.0, op0=OP.mult, op1=OP.add),
              [b_ra[q], b_zer], [b_asc[q]])
            V(lambda e, g=g, q=q: e.tensor_copy(out=ends[:, g:g + 1], in_=asc[q][:, T - 1:T]), [b_asc[q]], [b_ends])
            S.dma("sync", lambda e, g=g, q=q: e.dma_start(out=ac_d.ap()[g], in_=asc[q]), reads=[b_asc[q]], writes=[b_acd[g]])
        b_gin2, b_gout2 = Buf(), Buf()
        S.dma("sync", lambda e: e.dma_start(out=gin2.ap(), in_=ends), reads=[b_ends], writes=[b_gin2])
        POOL(lambda e: e.collective_compute("AllGather", OP.bypass, replica_groups=RG,
                                            ins=[gin2.ap()], outs=[gout2.ap()]), [b_gin2], [b_gout2])
        S.barrier()
        checkpoint(5, [(ends, 0)])

        A.off = B + 48 * KB
        sinbf = A.alloc([H, 256], BF16)
        coef = A.alloc([32], F32)
        m2_after = A.off
        tall = A.alloc([4, H, 256], F32)
        sacc = A.alloc([H, 256], F32)
        b_sin = Buf("sin")
        S.dma("sync", lambda e: e.dma_start(out=coef, in_=coef_d), writes=[b_sin])
        g1v = gout1.ap().rearrange("(r h d) e -> d r h e", r=4, h=H, d=128)
        for r in range(4):
            S.dma("sync", lambda e, r=r: e.dma_start(out=tall[:, r], in_=g1v[:, r]), reads=[b_gout1], writes=[b_sin])
        for r in range(4):
            cb = coef[:, r * 8:(r + 1) * 8].unsqueeze(2).to_broadcast([128, H, 256])
            if r == 0:
                V(lambda e, cb=cb: e.tensor_tensor(out=sacc, in0=tall[:, 0], in1=cb, op=OP.mult), [b_sin], [b_sin])
            else:
                V(lambda e, cb=cb, r=r: e.tensor_tensor(out=tall[:, r], in0=tall[:, r], in1=cb, op=OP.mult), [b_sin], [b_sin])
                V(lambda e, r=r: e.tensor_tensor(out=sacc, in0=sacc, in1=tall[:, r], op=OP.add), [b_sin], [b_sin])
        V(lambda e: e.tensor_copy(out=sinbf, in_=sacc), [b_sin], [b_sin])
        S.barrier()
        checkpoint(2, [(sacc.rearrange('p a b -> p (a b)'), 0)])

        A.off = m2_after
        ret_off = (A.off + 63) // 64 * 64
        retout = A.alloc([KC, T], BF16)
        osum = [A.alloc([NT, 256], F32) for _ in range(2)]
        sg = [A.alloc([NT, 256], BF16) for _ in range(2)]
        bst = A.alloc([NT, 6], F32)
        mv = A.alloc([NT, 2], F32)
        rsd = A.alloc([NT], F32)
        ynorm = [A.alloc([256], F32) for _ in range(2)]
        rtm = [A.alloc([256], BF16) for _ in range(2)]
        b_ret = [Buf() for _ in range(NT)]
        b_osum = [[Buf() for _ in range(NT)] for _ in range(2)]
        b_sg = [[Buf() for _ in range(NT)] for _ in range(2)]
        b_st = Buf()
        b_yn = [Buf(), Buf()]
        b_rtm = [Buf(), Buf()]
        slab_of_pair = {}

        def m2_load(hp):
            i = next_slab()
            slab_of_pair[hp] = i
            wload(i, slab3(i, 512), wcols(w_in, 4096 + hp * 512, 512))

        m2_load(0)
        for h in range(H):
            if h % 2 == 1 and h // 2 + 1 < 4:
                m2_load(h // 2 + 1)
            i = slab_of_pair[h // 2]
            s3 = slab3(i, 512)
            p = h % 2
            for tt in range(NT):
                bk = tt % 2
                c0 = 128 + tt * 128
                pg = pbuf[bk][0]
                gb = bank(bk, 256)
                mm(gb, [(hn[:, kc, c0:c0 + 128], s3[:, kc, p * 256:(p + 1) * 256]) for kc in range(KC)],
                   [b_hn[tt + 1], b_slab[i]], [pg])
                ACT(lambda e, gb=gb, tt=tt, p=p: e.activation(out=sg[p][:, tt, :], in_=gb, func=AF.Silu), [pg], [b_sg[p][tt]])
                pc = pbuf[4 + bk][0]
                cbk = bank(4 + bk, 256)
                mm(cbk, [(qts[:, h, tt * 128:(tt + 1) * 128], sinbf[:, h, :])], [b_sin], [pc])
                V(lambda e, cbk=cbk, tt=tt, p=p, h=h: e.scalar_tensor_tensor(
                    out=osum[p][:, tt, :], in0=cbk, scalar=float(gam[h] ** (128 * tt)),
                    in1=oloc[:, tt, h * 256:(h + 1) * 256], op0=OP.mult, op1=OP.add), [pc], [b_osum[p][tt]])
                V(lambda e, tt=tt, p=p: e.bn_stats(out=bst[:, tt, :], in_=osum[p][:, tt, :]), [b_osum[p][tt]], [b_st])
                V(lambda e, tt=tt: e.bn_aggr(out=mv[:, tt, :], in_=bst[:, tt, :]), [b_st], [b_st])
            ACT(lambda e: e.activation(out=rsd, in_=mv[:, :, 1], func=AF.Sqrt, bias=epsG), [b_st, b_const], [b_st])
            V(lambda e: e.reciprocal(out=rsd, in_=rsd), [b_st], [b_st])
            for tt in range(NT):
                q = tt % 2
                V(lambda e, tt=tt, q=q, p=p: e.tensor_scalar(out=ynorm[q], in0=osum[p][:, tt, :], scalar1=mv[:, tt, 0:1],
                                                             scalar2=rsd[:, tt:tt + 1], op0=OP.subtract, op1=OP.mult),
                  [b_osum[p][tt], b_st], [b_yn[q]])
                V(lambda e, tt=tt, q=q, p=p: e.tensor_tensor(out=rtm[q], in0=ynorm[q], in1=sg[p][:, tt, :], op=OP.mult),
                  [b_yn[q], b_sg[p][tt]], [b_rtm[q]])
                pv = psb[:, (2 + q) * 1024:(2 + q) * 1024 + 256]
                ptb = pbuf[2 + q][0]

                def tfn(e, q=q, pv=pv):
                    e.transpose(pv[:, 0:128], rtm[q][:, 0:128], ident)
                    return e.transpose(pv[:, 128:256], rtm[q][:, 128:256], ident)
                S.op("tensor", tfn, [b_rtm[q], b_const], [ptb])
                for ec in range(2):
                    kc = h * 2 + ec
                    ACT(lambda e, pv=pv, ec=ec, kc=kc, tt=tt: e.activation(
                        out=retout[:, kc, tt * 128:(tt + 1) * 128], in_=pv[:, ec * 128:(ec + 1) * 128],
                        func=AF.Copy, scale=vcol(V_GN, kc)), [ptb, b_const], [b_ret[tt]])
        S.barrier()
        checkpoint(3, [(retout[:, kc, :], kc * 128) for kc in range(4)] + [(retout[:, 15, :], 512)])

        A.off = B
        mg = A.alloc([KC, T], BF16)
        b_mg = [[Buf() for _ in range(2)] for _ in range(KC)]
        hn_half = [[b_hn[1 + th * 4 + q] for q in range(4)] for th in range(2)]

        def branch_gate(src, Wp, gcol0, accumulate, sgt, tmpm):
            b_sgt = [Buf(), Buf()]
            b_tmpm = [Buf(), Buf()]

            def load(sp):
                i = next_slab()
                s3 = slab3(i, 512)
                wload(i, s3[:, :, 0:256], wcols(Wp, sp * 256, 256))
                wload(i, s3[:, :, 256:512], wcols(w_in, gcol0 + sp * 256, 256))
                return i
            cur = load(0)
            for sp in range(8):
                nxt = load(sp + 1) if sp + 1 < 8 else None
                s3 = slab3(cur, 512)
                for oq in range(2):
                    oc = sp * 2 + oq
                    for th in range(2):
                        k = th
                        p1 = pbuf[k * 2][0]
                        p2 = pbuf[k * 2 + 1][0]
                        tk = slice(th * 512, (th + 1) * 512)
                        hk = slice(128 + th * 512, 128 + (th + 1) * 512)
                        mm(bank(k * 2), [(s3[:, kc, oq * 128:(oq + 1) * 128], src[:, kc, tk]) for kc in range(KC)],
                           [b_slab[cur]], [p1])
                        mm(bank(k * 2 + 1), [(s3[:, kc, 256 + oq * 128:256 + (oq + 1) * 128], hn[:, kc, hk])
                                             for kc in range(KC)], [b_slab[cur]] + hn_half[th], [p2])
                        ACT(lambda e, k=k: e.activation(out=sgt[k], in_=bank(k * 2 + 1), func=AF.Sigmoid), [p2], [b_sgt[k]])
                        if not accumulate:
                            V(lambda e, k=k, oc=oc, tk=tk: e.tensor_tensor(out=mg[:, oc, tk], in0=bank(k * 2), in1=sgt[k],
                                                                           op=OP.mult), [p1, b_sgt[k]], [b_mg[oc][th]])
                        else:
                            V(lambda e, k=k: e.tensor_tensor(out=tmpm[k], in0=bank(k * 2), in1=sgt[k], op=OP.mult),
                              [p1, b_sgt[k]], [b_tmpm[k]])
                            V(lambda e, k=k, oc=oc, tk=tk: e.tensor_tensor(out=mg[:, oc, tk], in0=tmpm[k], in1=mg[:, oc, tk],
                                                                           op=OP.add), [b_tmpm[k]], [b_mg[oc][th]])
                cur = nxt

        sgt3 = [A.alloc([512], F32) for _ in range(2)]
        tmp3 = [A.alloc([512], F32) for _ in range(2)]
        assert A.off <= ret_off
        branch_gate(retout, w_pret, 10240, False, sgt3, tmp3)
        S.barrier()
        checkpoint(4, [(mg[:, kc, :], kc * 128) for kc in range(8)])


        A.off = B + 32 * KB
        rnnout = A.alloc([KC, T], BF16)
        g2 = A.alloc([4, 32], F32)
        mskl = A.alloc([4], F32)
        hin = A.alloc([16], F32)
        tmpl = A.alloc([16], F32)
        hl = [A.alloc([T], F32) for _ in range(2)]
        al = [A.alloc([T], F32) for _ in range(2)]
        t1 = A.alloc([T], F32)
        gy = A.alloc([T], F32)
        b_l = Buf()
        b_hl = [Buf(), Buf()]
        b_al = [Buf(), Buf()]
        b_t1, b_gy = Buf(), Buf()
        b_rnn = [Buf() for _ in range(KC)]
        S.dma("sync", lambda e: e.dma_start(out=g2, in_=gout2.ap().rearrange("(r p) c -> p r c", r=4)), reads=[b_gout2], writes=[b_l])
        S.dma("sync", lambda e: e.dma_start(out=mskl, in_=mskl_d), writes=[b_l])
        V(lambda e: e.memset(hin, 0.0), [], [b_l])
        for r in range(4):
            V(lambda e, r=r: e.tensor_tensor(out=tmpl, in0=g2[:, r, 0:16], in1=hin, op=OP.mult), [b_l], [b_l])
            V(lambda e, r=r: e.tensor_tensor(out=tmpl, in0=tmpl, in1=g2[:, r, 16:32], op=OP.add), [b_l], [b_l])
            V(lambda e: e.tensor_tensor(out=tmpl, in0=tmpl, in1=hin, op=OP.subtract), [b_l], [b_l])
            V(lambda e, r=r: e.scalar_tensor_tensor(out=hin, in0=tmpl, scalar=mskl[:, r:r + 1], in1=hin, op0=OP.mult, op1=OP.add),
              [b_l], [b_l])
        cur = None
        for g in range(KC):
            if g % 4 == 0:
                cur = next_slab()
                wload(cur, slab3(cur, 512), wcols(w_in, 8192 + g * 128, 512))
            s3 = slab3(cur, 512)
            wsl = slice((g % 4) * 128, (g % 4 + 1) * 128)
            q = g % 2
            S.dma("sync", lambda e, g=g, q=q: e.dma_start(out=hl[q], in_=hs_d.ap()[g]), reads=[b_hsd[g]], writes=[b_hl[q]])
            S.dma("sync", lambda e, g=g, q=q: e.dma_start(out=al[q], in_=ac_d.ap()[g]), reads=[b_acd[g]], writes=[b_al[q]])
            pY = [pbuf[q * 2][0], pbuf[q * 2 + 1][0]]
            for th in range(2):
                mm(bank(q * 2 + th), [(s3[:, kc, wsl], hn[:, kc, 128 + th * 512:128 + (th + 1) * 512]) for kc in range(KC)],
                   [b_slab[cur]] + hn_half[th], [pY[th]])
            yv = ps[:, q * 1024:(q + 1) * 1024]
            ACT(lambda e, yv=yv: e.activation(out=t1, in_=yv, func=AF.Square), pY, [b_t1])
            V(lambda e: e.tensor_scalar(out=t1, in0=t1, scalar1=0.044715, scalar2=1.0, op0=OP.mult, op1=OP.add), [b_t1], [b_t1])
            V(lambda e, yv=yv: e.tensor_tensor(out=t1, in0=yv, in1=t1, op=OP.mult), pY + [b_t1], [b_t1])
            ACT(lambda e: e.activation(out=t1, in_=t1, func=AF.Sigmoid, scale=1.5957691216057308), [b_t1], [b_t1])
            V(lambda e, yv=yv: e.tensor_tensor(out=gy, in0=yv, in1=t1, op=OP.mult), pY + [b_t1], [b_gy])
            V(lambda e, g=g, q=q: e.scalar_tensor_tensor(out=hl[q], in0=al[q], scalar=hin[:, g:g + 1], in1=hl[q],
                                                         op0=OP.mult, op1=OP.add), [b_al[q], b_hl[q], b_l], [b_hl[q]])
            V(lambda e, g=g, q=q: e.tensor_tensor(out=rnnout[:, g, :], in0=gy, in1=hl[q], op=OP.mult),
              [b_gy, b_hl[q]], [b_rnn[g]])
        S.barrier()
        checkpoint(6, [(rnnout[:, kc, :], kc * 128) for kc in range(8)])

        A.off = B + 64 * KB
        sgt6 = [A.alloc([512], F32) for _ in range(2)]
        tmp6 = [A.alloc([512], F32) for _ in range(2)]
        branch_gate(rnnout, w_prnn, 12288, True, sgt6, tmp6)
        S.barrier()
        checkpoint(7, [(mg[:, kc, :], kc * 128) for kc in range(8)])

        A.off = B + 40 * KB
        hres = A.alloc([NT, D], F32)
        h_top = A.off
        b_h = [[Buf() for _ in range(4)] for _ in range(NT)]
        for tt in range(NT):
            S.dma("sync", lambda e, tt=tt: e.dma_start(out=hres[:, tt, :], in_=x_d[tt * 128:(tt + 1) * 128, :]),
                  writes=b_h[tt])

        def tm_proj_acc(act, nk, wsrc_fn, nslabs):
            def load(cs):
                i = next_slab()
                dst = slab[i][:, 0:nk * 512].rearrange("p (k n) -> p k n", k=nk, n=512)
                wload(i, dst, wsrc_fn(cs))
                return i, dst
            cur = load(0)
            for cs in range(nslabs):
                nxt = load(cs + 1) if cs + 1 < nslabs else None
                i, w3 = cur
                for tt in range(NT):
                    k = 4 + (tt % 2)
                    pb = pbuf[k][0]
                    mm(bank(k), [(act[:, kk, tt * 128:(tt + 1) * 128], w3[:, kk, :]) for kk in range(nk)], [b_slab[i]], [pb])
                    V(lambda e, k=k, tt=tt, cs=cs: e.tensor_tensor(out=hres[:, tt, cs * 512:(cs + 1) * 512], in0=bank(k),
                                                                   in1=hres[:, tt, cs * 512:(cs + 1) * 512], op=OP.add),
                      [pb], [b_h[tt][cs]])
                cur = nxt

        tm_proj_acc(mg, KC, lambda cs: wcols(w_mix, cs * 512, 512), 4)
        S.barrier()
        checkpoint(8, [(hres[:, tt, :], tt * 128) for tt in range(NT)])

        def h_norm(wbase):
            A.off = B
            ntx = norm_tmp()
            assert A.off <= B + 40 * KB
            tiles_ = [(hres[:, tt, :], None, 128 + tt * 128, b_hn[tt + 1]) for tt in range(NT)]
            for i, (src, _, col0, hb) in enumerate(tiles_):
                rmsnorm_to_fm([(src, Buf(), col0, hb)], wbase, None,
                              {**ntx, "ss": ntx["ss"][:, i:i + 1], "rstd": ntx["rstd"][:, i:i + 1]})
            return ntx

        ntx = h_norm(V_NXA)
        memst = A.alloc([2, D], F32)
        assert A.off <= B + 40 * KB
        A.off = h_top
        memn = A.alloc([KC, 256], BF16)
        b_memst = Buf()
        b_memn = Buf()
        S.dma("sync", lambda e: e.dma_start(out=memst, in_=mem_d.rearrange("(a p) d -> p a d", p=128)), writes=[b_memst])
        for mt in range(2):
            src = memst[:, mt, :]
            ssv = ntx["ss"][:, 8 + mt:9 + mt]
            rsv = ntx["rstd"][:, 8 + mt:9 + mt]
            jb = ntx["b_junk"][mt]
            xb = ntx["b_xs"][mt]
            xs = ntx["xs"][mt]
            b_ss = Buf()
            ACT(lambda e, src=src, mt=mt, ssv=ssv: e.activation(out=ntx["junk"][mt], in_=src, func=AF.Square, accum_out=ssv),
                [b_memst], [jb, b_ss])
            ACT(lambda e, ssv=ssv, rsv=rsv: e.activation(out=rsv, in_=ssv, func=AF.Sqrt, scale=1.0 / D, bias=epsN),
                [b_ss, b_const], [b_ss])
            V(lambda e, rsv=rsv: e.reciprocal(out=rsv, in_=rsv), [b_ss], [b_ss])
            ACT(lambda e, src=src, xs=xs, rsv=rsv: e.activation(out=xs, in_=src, func=AF.Copy, scale=rsv), [b_memst, b_ss], [xb])
            for g4 in range(4):
                pb = pbuf[2 + g4][0]
                pv = psb[:, (2 + g4) * 1024:(2 + g4) * 1024 + 512]

                def tfn(e, xs=xs, g4=g4, pv=pv):
                    ins = None
                    for qq in range(4):
                        kc = g4 * 4 + qq
                        ins = e.transpose(pv[:, qq * 128:(qq + 1) * 128], xs[:, kc * 128:(kc + 1) * 128], ident)
                    return ins
                S.op("tensor", tfn, [xb, b_const], [pb])
                wv = vecs[:, V_NMEM + g4 * 4: V_NMEM + g4 * 4 + 4].unsqueeze(2).to_broadcast([128, 4, 128])
                V(lambda e, pv=pv, g4=g4, mt=mt, wv=wv: e.tensor_tensor(
                    out=memn[:, g4 * 4:g4 * 4 + 4, mt * 128:(mt + 1) * 128],
                    in0=pv.rearrange("p (a b) -> p a b", a=4, b=128), in1=wv, op=OP.mult), [pb, b_const], [b_memn])
        S.barrier()
        checkpoint(9, [(memn.rearrange('p a b -> p (a b)')[:, 0:2048], 0), (hn[:, 0, 0:1152], 128)])

        A.off = B
        kth = A.alloc([4, 256], BF16)
        vh = A.alloc([2, 512], BF16)
        qT = A.alloc([4, T], BF16)
        oT = A.alloc([4, T], BF16)
        pT = [[A.alloc([512], BF16) for _ in range(2)] for _ in range(2)]
        rs = [A.alloc([512], F32) for _ in range(2)]
        assert A.off <= B + 40 * KB
        b_kth, b_vh = Buf(), Buf()
        b_qT = [[Buf() for _ in range(2)] for _ in range(4)]
        b_oT = [[Buf() for _ in range(2)] for _ in range(4)]
        b_pT = [[Buf() for _ in range(2)] for _ in range(2)]
        b_rs = [Buf(), Buf()]
        SCALE = 512.0 ** -0.5
        for hd in range(4):
            ik = next_slab()
            wload(ik, slab3(ik, 512), wcols(w_xk, hd * 512, 512))
            sk = slab3(ik, 512)
            for dc in range(4):
                pb = pbuf[dc % 2][0]
                mm(bank(dc % 2, 256), [(sk[:, kc, dc * 128:(dc + 1) * 128], memn[:, kc, :]) for kc in range(KC)],
                   [b_slab[ik], b_memn], [pb])
                ACT(lambda e, dc=dc: e.copy(out=kth[:, dc, :], in_=bank(dc % 2, 256)), [pb], [b_kth])
            iv = next_slab()
            wload(iv, slab3(iv, 512), wcols(w_xv, hd * 512, 512))
            sv = slab3(iv, 512)
            for mt in range(2):
                pb = pbuf[2 + mt][0]
                mm(bank(2 + mt), [(memn[:, kc, mt * 128:(mt + 1) * 128], sv[:, kc, :]) for kc in range(KC)],
                   [b_slab[iv], b_memn], [pb])
                ACT(lambda e, mt=mt: e.copy(out=vh[:, mt, :], in_=bank(2 + mt)), [pb], [b_vh])
            iq = next_slab()
            wload(iq, slab3(iq, 512), wcols(w_xq, hd * 512, 512))
            sq = slab3(iq, 512)
            for dc in range(4):
                for th in range(2):
                    k = 4 + (dc * 2 + th) % 2
                    pb = pbuf[k][0]
                    mm(bank(k), [(sq[:, kc, dc * 128:(dc + 1) * 128], hn[:, kc, 128 + th * 512:128 + (th + 1) * 512])
                                 for kc in range(KC)], [b_slab[iq]] + hn_half[th], [pb])
                    ACT(lambda e, k=k, dc=dc, th=th: e.copy(out=qT[:, dc, th * 512:(th + 1) * 512], in_=bank(k)),
                        [pb], [b_qT[dc][th]])
            for th in range(2):
                tk = slice(th * 512, (th + 1) * 512)
                for mt in range(2):
                    pb = pbuf[mt][0]
                    mm(bank(mt), [(kth[:, dc, mt * 128:(mt + 1) * 128], qT[:, dc, tk]) for dc in range(4)],
                       [b_kth] + [b_qT[dc][th] for dc in range(4)], [pb])
                    ACT(lambda e, mt=mt, th=th: e.activation(out=pT[th][mt], in_=bank(mt), func=AF.Exp, scale=SCALE),
                        [pb], [b_pT[th][mt]])
                pbs = pbuf[2][0]
                mm(bank(2), [(ones, pT[th][mt]) for mt in range(2)], [b_const, b_pT[th][0], b_pT[th][1]], [pbs])
                V(lambda e, th=th: e.reciprocal(out=rs[th], in_=bank(2)), [pbs], [b_rs[th]])
                for ec in range(4):
                    k = 4 + ec % 2
                    pb = pbuf[k][0]
                    mm(bank(k), [(vh[:, mt, ec * 128:(ec + 1) * 128], pT[th][mt]) for mt in range(2)],
                       [b_vh, b_pT[th][0], b_pT[th][1]], [pb])
                    V(lambda e, k=k, ec=ec, tk=tk, th=th: e.tensor_tensor(out=oT[:, ec, tk], in0=bank(k), in1=rs[th], op=OP.mult),
                      [pb, b_rs[th]], [b_oT[ec][th]])
            io = next_slab()
            wo3 = slab[io][:, 0:4 * 2048].rearrange("p (k n) -> p k n", k=4, n=2048)
            wload(io, wo3, w_xo[hd * 512:(hd + 1) * 512, :].rearrange("(k p) n -> p k n", p=128))
            for cs in range(4):
                for tt in range(NT):
                    k = 6 + (tt % 2)
                    pb = pbuf[k][0]
                    mm(bank(k), [(oT[:, kk, tt * 128:(tt + 1) * 128], wo3[:, kk, cs * 512:(cs + 1) * 512]) for kk in range(4)],
                       [b_slab[io]] + [b_oT[kk][tt // 4] for kk in range(4)], [pb])
                    V(lambda e, k=k, tt=tt, cs=cs: e.tensor_tensor(out=hres[:, tt, cs * 512:(cs + 1) * 512], in0=bank(k),
                                                                   in1=hres[:, tt, cs * 512:(cs + 1) * 512], op=OP.add),
                      [pb], [b_h[tt][cs]])
        S.barrier()
        checkpoint(10, [(hres[:, tt, :], tt * 128) for tt in range(NT)])

        h_norm(V_NFFN)
        A.off = h_top
        hal = A.alloc([32], F32)
        g3 = A.alloc([4, 32], F32)
        mskp = A.alloc([4], F32)
        hacc = A.alloc([32], F32)
        b_hal, b_gin3, b_gout3, b_g3 = Buf(), Buf(), Buf(), Buf()
        V(lambda e: e.tensor_copy(out=hal.rearrange("p (a b) -> p a b", a=16, b=2), in_=hn[:, :, 1150:1152]), [b_hn[8]], [b_hal])
        S.dma("sync", lambda e: e.dma_start(out=gin3.ap(), in_=hal), reads=[b_hal], writes=[b_gin3])
        POOL(lambda e: e.collective_compute("AllGather", OP.bypass, replica_groups=RG,
                                            ins=[gin3.ap()], outs=[gout3.ap()]), [b_gin3], [b_gout3])
        S.dma("sync", lambda e: e.dma_start(out=g3, in_=gout3.ap().rearrange("(r p) c -> p r c", r=4)), reads=[b_gout3], writes=[b_g3])
        S.dma("sync", lambda e: e.dma_start(out=mskp, in_=mskp_d), writes=[b_g3])
        V(lambda e: e.tensor_scalar(out=hacc, in0=g3[:, 0, :], scalar1=mskp[:, 0:1], scalar2=None, op0=OP.mult), [b_g3], [b_g3])
        for r in range(1, 4):
            V(lambda e, r=r: e.scalar_tensor_tensor(out=hacc, in0=g3[:, r, :], scalar=mskp[:, r:r + 1], in1=hacc,
                                                    op0=OP.mult, op1=OP.add), [b_g3], [b_g3])
        V(lambda e: e.tensor_copy(out=hn[:, :, 126:128], in_=hacc.rearrange("p (a b) -> p a b", a=16, b=2)), [b_g3], [b_hn[0]])
        S.barrier()
        checkpoint(11, [(hn[:, 0, 0:1152], 0), (hn[:, 15, 0:1152], 128)])

        A.off = B
        act = [A.alloc([4, T], BF16) for _ in range(2)]
        gext = A.alloc([1032], F32)
        vext = A.alloc([1032], F32)
        gcv = A.alloc([T], F32)
        vcv = A.alloc([T], F32)
        assert A.off <= B + 40 * KB
        b_act = [[Buf() for _ in range(4)] for _ in range(2)]
        b_gext, b_vext, b_gcv, b_vcv = Buf(), Buf(), Buf(), Buf()
        hn_all = [b_hn[i] for i in range(9)]
        cur = None
        for c in range(NFC):
            if c % 2 == 0:
                cur = next_slab()
                s3 = slab3(cur, 512)
                for q2 in range(2):
                    wload(cur, s3[:, :, q2 * 256:q2 * 256 + 128], wcols(w_up, (c + q2) * 128, 128))
                    wload(cur, s3[:, :, q2 * 256 + 128:q2 * 256 + 256], wcols(w_up, DFF + (c + q2) * 128, 128))
            s3 = slab3(cur, 512)
            grp, ci = divmod(c, 4)
            ap_ = grp % 2
            for (u, extb, b_e, pbase, hoff) in ((0, gext, b_gext, 0, 0), (1, vext, b_vext, 2, 0)):
                wsl = slice((c % 2) * 256 + u * 128, (c % 2) * 256 + u * 128 + 128)
                pA, pBk, pC = pbuf[pbase][0], pbuf[pbase + 1][0], pbuf[6 + u][0]
                mm(bank(pbase), [(s3[:, kc, wsl], hn[:, kc, 126:638]) for kc in range(KC)], [b_slab[cur]] + hn_all, [pA])
                mm(bank(pbase + 1), [(s3[:, kc, wsl], hn[:, kc, 638:1150]) for kc in range(KC)], [b_slab[cur]] + hn_all, [pBk])
                mm(bank(6 + u, 2, hoff), [(s3[:, kc, wsl], hn[:, kc, 1150:1152]) for kc in range(KC)], [b_slab[cur]] + hn_all, [pC])
                ACT(lambda e, extb=extb, pbase=pbase: e.copy(out=extb[:, 0:1024], in_=ps[:, pbase * 512:pbase * 512 + 1024]),
                    [pA, pBk], [b_e])
                ACT(lambda e, extb=extb, hoff=hoff, u=u: e.copy(out=extb[:, 1024:1026], in_=bank(6 + u, 2, hoff)), [pC], [b_e])
            for (u, extb, b_e, cv, b_cv) in ((0, gext, b_gext, gcv, b_gcv), (1, vext, b_vext, vcv, b_vcv)):
                col = u * NFC + c
                V(lambda e, extb=extb, cv=cv, col=col: e.tensor_scalar(
                    out=cv, in0=extb[:, 0:1024], scalar1=vecs[:, V_FCW + col:V_FCW + col + 1],
                    scalar2=vecs[:, V_FCB + col:V_FCB + col + 1], op0=OP.mult, op1=OP.add), [b_e, b_const], [b_cv])
                for j in range(1, 3):
                    V(lambda e, extb=extb, cv=cv, col=col, j=j: e.scalar_tensor_tensor(
                        out=cv, in0=extb[:, j:j + 1024], scalar=vecs[:, V_FCW + 88 * j + col:V_FCW + 88 * j + col + 1],
                        in1=cv, op0=OP.mult, op1=OP.add), [b_e, b_const], [b_cv])
            ACT(lambda e: e.activation(out=gcv, in_=gcv, func=AF.Silu), [b_gcv], [b_gcv])
            V(lambda e, ap_=ap_, ci=ci: e.tensor_tensor(out=act[ap_][:, ci, :], in0=gcv, in1=vcv, op=OP.mult),
              [b_gcv, b_vcv], [b_act[ap_][ci]])
            if ci == 3:
                io = next_slab()
                wd3 = slab[io][:, 0:4 * 2048].rearrange("p (k n) -> p k n", k=4, n=2048)
                wload(io, wd3, w_dn[grp * 512:(grp + 1) * 512, :].rearrange("(k p) n -> p k n", p=128))
                for cs in range(4):
                    for tt in range(NT):
                        k = 4 + (tt % 2)
                        pb = pbuf[k][0]
                        mm(bank(k), [(act[ap_][:, kk, tt * 128:(tt + 1) * 128], wd3[:, kk, cs * 512:(cs + 1) * 512])
                                     for kk in range(4)], [b_slab[io]] + b_act[ap_], [pb])
                        V(lambda e, k=k, tt=tt, cs=cs: e.tensor_tensor(out=hres[:, tt, cs * 512:(cs + 1) * 512], in0=bank(k),
                                                                       in1=hres[:, tt, cs * 512:(cs + 1) * 512], op=OP.add),
                          [pb], [b_h[tt][cs]])
        S.barrier()
        checkpoint(12, [(hres[:, tt, :], tt * 128) for tt in range(NT)])

        A.off = B
        fnw = A.alloc([D], F32)
        ot = [A.alloc([D], F32) for _ in range(2)]
        junk = A.alloc([D], BF16)
        ssf = A.alloc([NT], F32)
        assert A.off <= B + 40 * KB
        b_fnw, b_ssf, b_junk = Buf(), Buf(), Buf()
        b_ot = [Buf(), Buf()]
        b_out = [Buf() for _ in range(NT)]
        S.dma("sync", lambda e: e.dma_start(out=fnw, in_=fnw_d), writes=[b_fnw])
        for tt in range(NT):
            ACT(lambda e, tt=tt: e.activation(out=junk, in_=hres[:, tt, :], func=AF.Square, accum_out=ssf[:, tt:tt + 1]),
                [], [b_junk, b_ssf])
        ACT(lambda e: e.activation(out=ssf, in_=ssf, func=AF.Sqrt, scale=1.0 / D, bias=epsN), [b_ssf, b_const], [b_ssf])
        V(lambda e: e.reciprocal(out=ssf, in_=ssf), [b_ssf], [b_ssf])
        for tt in range(NT):
            q = tt % 2
            V(lambda e, tt=tt, q=q: e.scalar_tensor_tensor(out=ot[q], in0=hres[:, tt, :], scalar=ssf[:, tt:tt + 1], in1=fnw,
                                                           op0=OP.mult, op1=OP.mult), [b_ssf, b_fnw], [b_ot[q]])
            S.dma("sync", lambda e, tt=tt, q=q: e.dma_start(out=out_d[tt * 128:(tt + 1) * 128, :], in_=ot[q]),
                  reads=[b_ot[q]], writes=[b_out[tt]])
        S.barrier()
        S.emit()
      except _Stop:
        pass
    return nc


_NC_CACHE = {}


def _tables():
    gam = np.array([1.0 - 2.0 ** (-5.0 - h) for h in range(H)], np.float64)
    n = np.arange(128, dtype=np.float64)
    rel = n[None, :] - n[:, None]
    dt = np.zeros((128, H, 128), np.float64)
    for h in range(H):
        dt[:, h, :] = np.where(rel >= 0, gam[h] ** np.maximum(rel, 0), 0.0) * (128.0 ** -0.5)
    qdec = np.zeros((128, H, 128), np.float64)
    for h in range(H):
        qdec[:, h, :] = (gam[h] ** (n + 1.0))[None, :]
    kdec = np.zeros((128, H), np.float64)
    for h in range(H):
        kdec[:, h] = gam[h] ** (127.0 - n) * (128.0 ** -0.5)
    invf = (np.float32(10000.0) ** (-(np.arange(64, dtype=np.float32) * np.float32(2.0) / np.float32(128.0)))).astype(np.float32)
    return (dt.reshape(128, H * 128).astype(np.float32), qdec.reshape(128, H * 128).astype(np.float32),
            kdec.astype(np.float32), np.tile(invf[None, :], (128, 1)).astype(np.float32), gam)


def _fm(v):
    v = np.asarray(v, np.float32).reshape(-1)
    return np.ascontiguousarray(v.reshape(-1, 128).T)


def kernel(x, mem, positions, norm_mix_w, w_in, ret_gn_w, rnn_conv_w, rnn_conv_b,
           rg_w_a, rg_b_a, rg_w_x, rg_b_x, rg_lambda, w_proj_ret, w_proj_rnn, w_mix_out,
           norm_xa_w, norm_mem_w, xa_w_q, xa_w_k, xa_w_v, xa_w_o,
           norm_ffn_w, ffn_w_up, ffn_conv_w, ffn_conv_b, ffn_w_down, final_norm_w):
    f = lambda a: np.ascontiguousarray(np.asarray(a, np.float32))
    x = f(x)
    mem = f(mem)
    positions = np.asarray(positions, np.int32)
    dt, qdec, kdec, invf, gam = _tables()
    vecs = np.zeros((128, NV), np.float32)
    vecs[:, V_NMIX:V_NMIX + 16] = _fm(norm_mix_w[0])
    cw = f(rnn_conv_w[0])
    for j in range(4):
        vecs[:, V_CW + 16 * j:V_CW + 16 * j + 16] = _fm(cw[j])
    vecs[:, V_CB:V_CB + 16] = _fm(rnn_conv_b[0])
    vecs[:, V_BA:V_BA + 16] = _fm(rg_b_a[0])
    vecs[:, V_BX:V_BX + 16] = _fm(rg_b_x[0])
    vecs[:, V_LAM:V_LAM + 16] = _fm(rg_lambda[0])
    vecs[:, V_NXA:V_NXA + 16] = _fm(norm_xa_w[0])
    vecs[:, V_NMEM:V_NMEM + 16] = _fm(norm_mem_w[0])
    vecs[:, V_NFFN:V_NFFN + 16] = _fm(norm_ffn_w[0])
    vecs[:, V_GN:V_GN + 16] = _fm(ret_gn_w[0])
    fcw = f(ffn_conv_w[0])
    for j in range(3):
        vecs[:, V_FCW + 88 * j:V_FCW + 88 * j + 88] = _fm(fcw[j])
    vecs[:, V_FCB:V_FCB + 88] = _fm(ffn_conv_b[0])
    fnw = np.ascontiguousarray(np.tile(f(final_norm_w)[None, :], (128, 1)))
    ident = np.eye(128, dtype=np.float32)
    shared = {
        "vecs": vecs, "invf": invf, "dtab": dt, "qdec": qdec, "kdec": kdec, "ident": ident, "fnw": fnw,
        "w_in": f(w_in[0]), "rg_w_a": f(rg_w_a[0]), "rg_w_x": f(rg_w_x[0]),
        "w_proj_ret": f(w_proj_ret[0]), "w_proj_rnn": f(w_proj_rnn[0]), "w_mix_out": f(w_mix_out[0]),
        "xa_w_q": f(xa_w_q[0]), "xa_w_k": f(xa_w_k[0]), "xa_w_v": f(xa_w_v[0]), "xa_w_o": f(xa_w_o[0]),
        "ffn_w_up": f(ffn_w_up[0]), "ffn_w_down": f(ffn_w_down[0]),
    }
    if SMALLW:
        for kname in (() if STOP >= 20 else ("w_in",)) + ("rg_w_a", "rg_w_x", "w_proj_ret", "w_proj_rnn", "w_mix_out", "xa_w_q", "xa_w_k",
                      "xa_w_v", "xa_w_o", "ffn_w_up", "ffn_w_down"):
            shared[kname] = np.zeros((1, 1), np.float32)
    in_maps = []
    for c in range(8):
        b, j = divmod(c, 4)
        s0 = j * T
        xh = x[b, s0 - 128:s0] if j > 0 else np.zeros((128, D), np.float32)
        coefr = np.zeros((4, H), np.float64)
        for r in range(4):
            if r < j:
                coefr[r] = gam ** (1024.0 * (j - 1 - r))
        maskl = np.array([1.0 if r < j else 0.0 for r in range(4)], np.float32)
        maskp = np.array([1.0 if r == j - 1 else 0.0 for r in range(4)], np.float32)
        m = dict(shared)
        m.update({
            "x": np.ascontiguousarray(x[b, s0:s0 + T]),
            "xh": np.ascontiguousarray(xh),
            "mem": np.ascontiguousarray(mem[b]),
            "pos": np.ascontiguousarray(positions[b, s0:s0 + T].reshape(NT, 128).T),
            "coefr": np.ascontiguousarray(np.tile(coefr.reshape(1, 32).astype(np.float32), (128, 1))),
            "maskl": np.ascontiguousarray(np.tile(maskl[None, :], (128, 1))),
            "maskp": np.ascontiguousarray(np.tile(maskp[None, :], (128, 1))),
        })
        in_maps.append(m)
    if "nc" not in _NC_CACHE:
        _NC_CACHE["nc"] = build_nc()
    res = run_bass_kernel_spmd(_NC_CACHE["nc"], in_maps, core_ids=list(range(8)))
    out = np.zeros((2, 4 * T, D), np.float32)
    for c in range(8):
        b, j = divmod(c, 4)
        out[b, j * T:(j + 1) * T] = res.results[c]["out"]
    return out
```
